# Optimizing a Trainium2 kernel written in Bass

```python
import math
import jax, jax.numpy as jnp
from jax import lax
import numpy as np

D_MODEL = 1024
BATCH = 4
SEQ = 4096
DEPTH = 1

ATTN_HEADS = 4
ATTN_HEAD_DIM = 64
ATTN_WIDTH = ATTN_HEADS * 2 * ATTN_HEAD_DIM
CONV_WIDTH = D_MODEL // 2
CONV_K = 3
Q_BLOCK = 128
NORM_EPS = 1e-6
LAMBDA_PARAM_STD = 0.1
IN_SPLITS = (
    ATTN_WIDTH,
    ATTN_WIDTH,
    ATTN_WIDTH,
    ATTN_WIDTH,
    CONV_WIDTH,
    CONV_WIDTH,
    CONV_WIDTH,
    CONV_WIDTH,
    D_MODEL,
    D_MODEL,
)
IN_COLS = sum(IN_SPLITS)

kernel_name = "hybrid_diffattn_shortconv_gated_merge"


def rms_norm(x, gain):
    x32 = x.astype(jnp.float32)
    y = x32 * lax.rsqrt(jnp.mean(x32 * x32, axis=-1, keepdims=True) + NORM_EPS)
    return (y * gain.astype(jnp.float32)).astype(x.dtype)


def alibi_slopes(n_heads):
    return jnp.asarray([2.0 ** (-8.0 * (i + 1) / n_heads) for i in range(n_heads)], dtype=jnp.float32)


def lambda_init_fn(layer_idx):
    return 0.8 - 0.6 * math.exp(-0.3 * layer_idx)


def diff_attention(q, k, v, lam, slopes):
    B, S, H, _, d = q.shape
    nb = S // Q_BLOCK
    scale = d ** -0.5
    qb = q.reshape(B, nb, Q_BLOCK, H, 2, d).transpose(1, 0, 3, 4, 2, 5)
    kt = k.transpose(0, 2, 3, 1, 4)
    vt = v.transpose(0, 2, 1, 3)
    kpos = jnp.arange(S)

    def block(args):
        qblk, blk = args
        qpos = blk * Q_BLOCK + jnp.arange(Q_BLOCK)
        s = jnp.einsum('bhmqd,bhmkd->bhmqk', qblk, kt).astype(jnp.float32) * scale
        dist = (qpos[:, None] - kpos[None, :]).astype(jnp.float32)
        s = s - slopes[:, None, None, None] * dist
        s = jnp.where(kpos[None, :] <= qpos[:, None], s, -jnp.inf)
        p = jax.nn.softmax(s, axis=-1)
        a = p[:, :, 0] - lam * p[:, :, 1]
        return jnp.einsum('bhqk,bhke->bhqe', a.astype(vt.dtype), vt)

    o = lax.map(block, (qb, jnp.arange(nb)))
    return o.transpose(1, 0, 3, 2, 4).reshape(B, S, H, 2 * d)


def causal_depthwise_conv(u, w):
    K, C = w.shape
    return lax.conv_general_dilated(
        u, w[:, None, :].astype(u.dtype), window_strides=(1,), padding=[(K - 1, 0)],
        dimension_numbers=('NWC', 'WIO', 'NWC'), feature_group_count=C)


def hybrid_layer(x, layer_idx, w_in, lambda_q1, lambda_k1, lambda_q2, lambda_k2, subln_gain,
                 conv_w, w_attn_o, w_conv_o, b_merge, w_out, g_pre, g_post):
    B, S, D = x.shape
    h = rms_norm(x, g_pre)
    proj = h @ w_in
    idx = list(np.cumsum(IN_SPLITS)[:-1])
    q, k, v, z_a, cb, cc, cx, z_c, ga, gc = jnp.split(proj, idx, axis=-1)

    lam_init = lambda_init_fn(layer_idx)
    lam = (jnp.exp(jnp.sum(lambda_q1.astype(jnp.float32) * lambda_k1.astype(jnp.float32)))
           - jnp.exp(jnp.sum(lambda_q2.astype(jnp.float32) * lambda_k2.astype(jnp.float32)))
           + lam_init)
    q = q.reshape(B, S, ATTN_HEADS, 2, ATTN_HEAD_DIM)
    k = k.reshape(B, S, ATTN_HEADS, 2, ATTN_HEAD_DIM)
    v = v.reshape(B, S, ATTN_HEADS, 2 * ATTN_HEAD_DIM)
    o = diff_attention(q, k, v, lam, alibi_slopes(ATTN_HEADS))
    o = rms_norm(o, subln_gain) * (1.0 - lam_init)
    y_attn = (o.reshape(B, S, ATTN_WIDTH) * jax.nn.silu(z_a)) @ w_attn_o

    u = causal_depthwise_conv(cc * cx, conv_w)
    y_conv = ((cb * u) * jax.nn.silu(z_c)) @ w_conv_o

    b_a, b_c = jnp.split(b_merge, 2)
    y = jax.nn.sigmoid(ga + b_a) * y_attn + jax.nn.sigmoid(gc + b_c) * y_conv
    out = y @ w_out
    return x + rms_norm(out, g_post)


def setup_inputs(seed: int = 0) -> dict:
    key = jax.random.key(seed)
    ks = jax.random.split(key, 16)
    f32 = jnp.float32
    nrm = lambda k, shape, s: jax.random.normal(k, shape, f32) * s
    return {
        "x": nrm(ks[0], (BATCH, SEQ, D_MODEL), 1.0),
        "w_in": nrm(ks[1], (DEPTH, D_MODEL, IN_COLS), D_MODEL ** -0.5),
        "lambda_q1": nrm(ks[2], (DEPTH, ATTN_HEAD_DIM), LAMBDA_PARAM_STD),
        "lambda_k1": nrm(ks[3], (DEPTH, ATTN_HEAD_DIM), LAMBDA_PARAM_STD),
        "lambda_q2": nrm(ks[4], (DEPTH, ATTN_HEAD_DIM), LAMBDA_PARAM_STD),
        "lambda_k2": nrm(ks[5], (DEPTH, ATTN_HEAD_DIM), LAMBDA_PARAM_STD),
        "subln_gain": 1.0 + nrm(ks[6], (DEPTH, 2 * ATTN_HEAD_DIM), 0.02),
        "conv_w": nrm(ks[7], (DEPTH, CONV_K, CONV_WIDTH), CONV_K ** -0.5),
        "w_attn_o": nrm(ks[8], (DEPTH, ATTN_WIDTH, D_MODEL), ATTN_WIDTH ** -0.5),
        "w_conv_o": nrm(ks[9], (DEPTH, CONV_WIDTH, D_MODEL), CONV_WIDTH ** -0.5),
        "b_merge": nrm(ks[10], (DEPTH, 2 * D_MODEL), 0.01),
        "w_out": nrm(ks[11], (DEPTH, D_MODEL, D_MODEL), D_MODEL ** -0.5),
        "g_pre": 1.0 + nrm(ks[12], (DEPTH, D_MODEL), 0.02),
        "g_post": 1.0 + nrm(ks[13], (DEPTH, D_MODEL), 0.02),
    }


def reference(x, w_in, lambda_q1, lambda_k1, lambda_q2, lambda_k2, subln_gain, conv_w,
              w_attn_o, w_conv_o, b_merge, w_out, g_pre, g_post):
    for l in range(DEPTH):
        x = hybrid_layer(x, l, w_in[l], lambda_q1[l], lambda_k1[l], lambda_q2[l], lambda_k2[l],
                         subln_gain[l], conv_w[l], w_attn_o[l], w_conv_o[l], b_merge[l],
                         w_out[l], g_pre[l], g_post[l])
    return x
```

```python
import contextlib
import math

import ml_dtypes
import numpy as np

import concourse.bass as bass
import concourse.mybir as mybir
from concourse.bass_utils import run_bass_kernel_spmd

F32 = mybir.dt.float32
BF16 = mybir.dt.bfloat16
AF = mybir.ActivationFunctionType
ALU = mybir.AluOpType

ENGS = ("sp", "act", "pe", "dve", "pool")

D = 1024
KC = 8
NT = 4096
NOWN = 2048
NEG = -262144.0
EPS = 1e-6
LAM_INIT = 0.8 - 0.6 * math.exp(-0.3 * 0)
SLOPES = [2.0 ** (-8.0 * (i + 1) / 4) for i in range(4)]


class Op:
    __slots__ = ("eng", "fn", "waits", "token", "needs_inc", "dma", "idx")

    def __init__(self, eng, fn, dma):
        self.eng = eng
        self.fn = fn
        self.waits = []
        self.needs_inc = False
        self.dma = dma
        self.token = None


class Prog:
    def __init__(self, nc, stack, same_engine_sync=("act", "dve", "pool")):
        self.nc = nc
        self.stack = stack
        self.ops = {e: [] for e in ENGS}
        self.sems = {}
        self.last_w = {}
        self.readers = {}
        self.dma_count = {}
        self.same_engine_sync = set(same_engine_sync)
        self.pending_barrier = {}

    def op(self, eng, fn, reads=(), writes=(), dma=None, extra=()):
        o = Op(eng, fn, dma)
        o.idx = len(self.ops[eng])
        deps = list(extra)
        if eng in self.pending_barrier:
            deps.extend(self.pending_barrier.pop(eng))
        for r in reads:
            w = self.last_w.get(r)
            if w is not None:
                deps.append(w)
        for r in writes:
            w = self.last_w.get(r)
            if w is not None:
                deps.append(w)
            deps.extend(self.readers.get(r, ()))
        latest = {}
        seen = set()
        for d in deps:
            if d is o or id(d) in seen:
                continue
            seen.add(id(d))
            if d.dma is None:
                if d.eng == eng and eng not in self.same_engine_sync:
                    continue
                if d.eng not in latest or d.idx > latest[d.eng].idx:
                    latest[d.eng] = d
            else:
                o.waits.append(d)
                d.needs_inc = True
        for d in latest.values():
            o.waits.append(d)
            d.needs_inc = True
        for r in reads:
            self.readers.setdefault(r, []).append(o)
        for r in writes:
            self.last_w[r] = o
            self.readers[r] = []
        if dma is not None:
            o.needs_inc = True
        self.ops[eng].append(o)
        return o

    def barrier(self):
        lasts = []
        for e in ENGS:
            if self.ops[e]:
                lasts.append(self.ops[e][-1])
            seen_streams = set()
            for o in reversed(self.ops[e]):
                if o.dma is not None and o.dma not in seen_streams:
                    seen_streams.add(o.dma)
                    lasts.append(o)
        for e in ENGS:
            self.pending_barrier[e] = list(lasts)

    def finalize(self):
        totals = {}
        for e in ENGS:
            for o in self.ops[e]:
                if o.dma is not None:
                    totals[o.dma] = totals.get(o.dma, 0) + 1
        for e in ENGS:
            c = 0
            for o in self.ops[e]:
                if o.dma is not None:
                    k = self.dma_count.get(o.dma, 0) + 1
                    self.dma_count[o.dma] = k
                    if o.dma.startswith("G:"):
                        k = totals[o.dma]
                    o.token = ("d_" + o.dma.replace(":", "_"), 16 * k)
                elif o.needs_inc:
                    c += 1
                    o.token = ("e_" + e, c)
        names = sorted(set(o.token[0] for e in ENGS for o in self.ops[e] if o.token))
        for n in names:
            self.sems[n] = self.stack.enter_context(self.nc.semaphore("s_" + n))

    def replay(self, eng_name, engine):
        waited = {}
        for o in self.ops[eng_name]:
            need = {}
            for d in o.waits:
                s, v = d.token
                if v > need.get(s, 0):
                    need[s] = v
            for s in sorted(need):
                v = need[s]
                if waited.get(s, 0) >= v:
                    continue
                engine.wait_ge(self.sems[s], v)
                waited[s] = v
            ins = o.fn(engine)
            if o.token is not None:
                ins.then_inc(self.sems[o.token[0]], 16 if o.dma is not None else 1)

    def run(self, final_ops=()):
        self.finalize()
        with self.nc.Block() as block:
            @block.sync
            def _(e):
                self.replay("sp", e)
                for o in final_ops:
                    e.wait_ge(self.sems[o.token[0]], o.token[1])

            @block.scalar
            def _(e):
                self.replay("act", e)

            @block.tensor
            def _(e):
                self.replay("pe", e)

            @block.vector
            def _(e):
                self.replay("dve", e)

            @block.gpsimd
            def _(e):
                self.replay("pool", e)


def build_program(debug=False):
    nc = bass.Bass("TRN2", target_bir_lowering=False)
    dbg_outs = []

    def dbg(p, name, ap_fn, shape, dt, reads):
        if not debug:
            return
        t = nc.dram_tensor("dbg_" + name, list(shape), dt, kind="ExternalOutput").ap()
        dbg_outs.append(p.op("sp", (lambda e: e.dma_start(out=t, in_=ap_fn())), reads=reads, dma=f"dbg{len(dbg_outs)}"))

    def din(name, shape, dt=F32):
        return nc.dram_tensor(name, list(shape), dt, kind="ExternalInput").ap()

    x_perm = din("x_perm", [NT, D])
    x_halo = din("x_halo", [8, D])
    w_in_r = din("w_in_r", [48, 128, KC * 128])
    wao_r = din("wao_r", [128, 4 * D])
    wco_r = din("wco_r", [128, 4 * D])
    wo_r = din("wo_r", [128, 8 * D])
    lamv = din("lamv", [1, 256])
    gsub = din("gsub", [128, 1])
    cw_d = din("cw", [128, 12])
    bm_d = din("bm", [128, 16])
    gb_d = din("gb", [128, KC * 128])
    gpost_d = din("gpost", [1, D])
    identb_d = din("identb", [128, 128], BF16)
    identf_d = din("identf", [128, 128])
    tri_d = din("tri", [128, 128], BF16)
    kaug_d = din("kaug", [8, NT], BF16)
    qaug_d = din("qaug", [32, NOWN], BF16)
    xbias_d = din("xbias", [128, 1])
    zeros_d = din("zeros", [64, NT], BF16)
    out_own = nc.dram_tensor("out_own", [NOWN, D], F32, kind="ExternalOutput").ap()

    with contextlib.ExitStack() as st:
        K = 1024
        ARENA_BYTES = 212736
        arena = st.enter_context(nc.sbuf_tensor("arena", [128, ARENA_BYTES // 2], BF16))
        banks = [st.enter_context(nc.psum_tensor(f"bank{i}", [128, 512], F32)) for i in range(8)]

        def reg(off, nbytes, dt):
            a = arena[:, off // 2:(off + nbytes) // 2]
            return a.bitcast(F32) if dt == F32 else a

        o = 0

        def take(n):
            nonlocal o
            r = o
            o += (n + 63) // 64 * 64
            return r

        identb = reg(take(256), 256, BF16)
        tri = reg(take(256), 256, BF16)
        identf = reg(take(512), 512, F32)
        lamt = reg(take(1024), 1024, F32)
        small = reg(take(1024), 1024, F32)
        cw = reg(take(48), 48, F32)
        bm = reg(take(64), 64, F32)
        gpre = reg(take(32), 32, F32)
        stats = reg(take(1024), 1024, F32)
        gb = reg(take(4096), 4096, F32).rearrange("p (a b) -> p a b", a=KC)
        assert o <= 9 * K, o
        o = 9 * K
        HT0 = o
        hT2 = [reg(take(32 * K), 32 * K, BF16).rearrange("p (a b) -> p a b", a=KC) for _ in range(2)]

        def hT_tile(kc, tt):
            return hT2[tt // 4][:, kc, (tt % 4) * 512:(tt % 4 + 1) * 512]

        def hT_blk(kc, i):
            return hT2[i // 16][:, kc, (i % 16) * 128:(i % 16 + 1) * 128]
        hTh = reg(take(128), 128, BF16).rearrange("p (a b) -> p a b", a=KC)
        base_ph = o
        V = reg(take(33024), 33024, BF16).rearrange("p (j h e) -> p j h e", j=32, h=4)
        kq_off = o
        KTa = [reg(take(8 * K), 8 * K, BF16) for _ in range(2)]
        QTa = [reg(take(4 * K), 4 * K, BF16) for _ in range(2)]
        NPT = 3
        PT = [[reg(take(K), K, BF16) for _ in range(NPT)] for _ in range(2)]
        oraw_off = take(32 * K)
        oraw = reg(oraw_off, 32 * K, F32).rearrange("p (h b e) -> p h b e", h=4, b=16)
        NXR = 8
        xring = [reg(oraw_off + i * 4 * K, 4 * K, F32) for i in range(NXR)]
        oTn = [reg(oraw_off + h_ * 8 * K, 8 * K, F32) for h_ in range(4)]
        ph01_tail = o
        wph0 = [reg(take(2 * K), 2 * K, BF16).rearrange("p (a b) -> p a b", a=KC) for _ in range(6)]
        ocp = [reg(ph01_tail + b_ * 4160, 4128, F32).rearrange("p (a e) -> p a e", a=8) for b_ in range(2)]
        xn = [reg(take(2 * K), 2 * K, BF16) for _ in range(3)]
        wst_off = [take(4 * K) for _ in range(3)]
        wst = [reg(wst_off[i], 4 * K, F32).rearrange("p (a b) -> p a b", a=KC) for i in range(3)]
        wb_extra = [reg(wst_off[2] + i * 2 * K, 2 * K, BF16).rearrange("p (a b) -> p a b", a=KC) for i in range(2)]
        wbt = [reg(take(2 * K), 2 * K, BF16).rearrange("p (a b) -> p a b", a=KC) for _ in range(3)]
        att_t = reg(take(1024), 1024, F32)
        junk = reg(take(2 * K), 2 * K, BF16)
        assert o <= ARENA_BYTES, o
        ontmp = junk.bitcast(F32).rearrange("p (q e) -> p q e", q=4)
        o = base_ph
        oT = reg(take(32 * K), 32 * K, F32).rearrange("p (c t) -> p c t", c=4)
        assert o <= kq_off
        o = HT0 + 32 * K
        gattnT = reg(take(16 * K), 16 * K, BF16).rearrange("p (c t) -> p c t", c=4)
        gconvT = reg(take(16 * K), 16 * K, BF16).rearrange("p (c t) -> p c t", c=4)
        assert o <= HT0 + 64 * K
        o = kq_off
        prod = [reg(take(4 * 514 * 4), 4 * 514 * 4, F32).rearrange("p (t n) -> p t n", t=4)]
        tmp = [reg(take(2 * K), 2 * K, F32) for _ in range(8)]
        gpost = reg(take(4 * K), 4 * K, F32)
        assert o <= oraw_off, (o, oraw_off)
        yT = reg(oraw_off, 32 * K, BF16).rearrange("p (j t) -> p j t", j=KC)
        o = ph01_tail
        wao = reg(take(8 * K), 8 * K, BF16).rearrange("p (c n) -> p c n", c=4)
        wco = reg(take(8 * K), 8 * K, BF16).rearrange("p (c n) -> p c n", c=4)
        assert o <= ph01_tail + 16 * K
        o = base_ph
        wo = reg(take(16 * K), 16 * K, BF16).rearrange("p (j n) -> p j n", j=KC)
        assert o <= base_ph + 32 * K
        NOS = 4
        ostage = [reg(HT0 + 32 * K + i * 4 * K, 4 * K, F32) for i in range(NOS)]
        xres = [reg(HT0 + 48 * K + i * 4 * K, 4 * K, F32) for i in range(NOS)]

        S_EPSD = small[:, 0:1]
        S_EPSO = small[:, 1:2]
        S_XB = small[:, 2:3]
        S_GCOL = small[:, 3:4]
        S_S1 = small[:, 4:5]
        S_S2 = small[:, 5:6]
        S_E1 = small[:, 6:7]
        S_E2 = small[:, 7:8]
        S_NLAM = small[:, 8:9]
        S_GS = small[:, 9:10]
        S_R = small[:, 16:32]

        ss0 = stats[:, 0:40]
        sd0 = stats[:, 40:80]
        rs0 = stats[:, 80:120]
        sso = stats[:, 120:184]
        sdo = stats[:, 184:248]
        p = Prog(nc, st)

        def bview_bf16(bank):
            return bank[:].bitcast(BF16)

        def ld(eng, out_ap, in_ap, res, stream):
            return p.op(eng, lambda e: e.dma_start(out=out_ap, in_=in_ap), writes=[res], dma=stream)

        for i0 in range(2):
            p.op("sp", (lambda e, i0=i0: e.dma_start(out=xring[i0], in_=x_perm[i0 * 128:(i0 + 1) * 128, :])),
                 writes=[("x", i0)], dma=f"ldx{i0}")
        ld("sp", gb, gb_d.rearrange("p (a b) -> p a b", a=KC), "gb", "ldgb")
        ld("sp", identb, identb_d, "identb", "ldib")
        ld("sp", tri, tri_d, "tri", "G:ldc")
        ld("sp", identf, identf_d, "identf", "G:ldc")
        ld("sp", lamt, lamv.partition_broadcast(128), "lamt", "G:ldc")
        ld("sp", cw[:, 0:12], cw_d, "cw", "G:ldc")
        ld("sp", bm[:, 0:16], bm_d, "bm", "G:ldc")
        ld("sp", S_XB, xbias_d, "xb", "G:ldc")
        ld("sp", S_GS, gsub, "gs", "G:ldc")
        p.op("pool", lambda e: e.memset(S_EPSD, EPS), writes=["epsd"])
        p.op("pool", lambda e: e.memset(S_EPSO, EPS), writes=["epso"])
        p.op("dve", lambda e: e.tensor_scalar(out=S_GCOL, in0=S_GS, scalar1=float(1.0 - LAM_INIT), scalar2=None, op0=ALU.mult),
             reads=["gs"], writes=["gcol"])
        p.op("dve", lambda e: e.scalar_tensor_tensor(out=junk.bitcast(F32)[:, 0:64], in0=lamt[:, 0:64], scalar=1.0, in1=lamt[:, 64:128],
                                                     op0=ALU.mult, op1=ALU.mult, accum_out=S_S1),
             reads=["lamt"], writes=["s1"])
        p.op("dve", lambda e: e.scalar_tensor_tensor(out=junk.bitcast(F32)[:, 64:128], in0=lamt[:, 128:192], scalar=1.0, in1=lamt[:, 192:256],
                                                     op0=ALU.mult, op1=ALU.mult, accum_out=S_S2),
             reads=["lamt"], writes=["s2"])
        p.op("act", lambda e: e.activation(out=S_E1, in_=S_S1, func=AF.Exp), reads=["s1"], writes=["e1"])
        p.op("act", lambda e: e.activation(out=S_E2, in_=S_S2, func=AF.Exp), reads=["s2"], writes=["e2"])
        p.op("dve", lambda e: e.tensor_tensor(out=S_NLAM, in0=S_E2, in1=S_E1, op=ALU.subtract), reads=["e1", "e2"], writes=["nl0"])
        p.op("dve", lambda e: e.tensor_scalar(out=S_NLAM, in0=S_NLAM, scalar1=float(-LAM_INIT), scalar2=None, op0=ALU.add),
             reads=["nl0"], writes=["nlam"])

        wctr = [0]

        def load_w_chunk(c, dst_ap, dst_res, eng="pool", xslot=None, defer=None):
            if xslot is None:
                s = wctr[0] % 3
                wctr[0] += 1
                stg, sres, strm = wst[s], ("wst", s), f"ldw{s}"
            else:
                stg, sres, strm = xring[xslot].rearrange("p (a b) -> p a b", a=KC), ("x", xslot), f"ldx{xslot}"
            p.op("sp", (lambda e, c=c, stg=stg: e.dma_start(out=stg, in_=w_in_r[c].rearrange("p (a b) -> p a b", a=KC))),
                 writes=[sres], dma=strm)
            if defer is not None:
                defer.append(lambda: cast_w_chunk(stg, sres, dst_ap, dst_res, eng))
                return
            cast_w_chunk(stg, sres, dst_ap, dst_res, eng)

        def cast_w_chunk(stg, sres, dst_ap, dst_res, eng):
            if eng == "act":
                for kc in range(KC):
                    p.op("act", (lambda e, stg=stg, dst_ap=dst_ap, kc=kc: e.activation(out=dst_ap[:, kc, :], in_=stg[:, kc, :], func=AF.Copy,
                                                                                      scale=gb[:, kc, 0:1])),
                         reads=[sres, "gb"], writes=[dst_res])
            else:
                p.op(eng, (lambda e, stg=stg, dst_ap=dst_ap: e.tensor_tensor(out=dst_ap, in0=stg, in1=gb, op=ALU.mult)),
                     reads=[sres, "gb"], writes=[dst_res])

        wv_all = reg(ph01_tail, 8 * K, BF16).rearrange("p (a b) -> p a b", a=KC)
        for c in range(3):
            load_w_chunk(8 + c, wv_all[:, :, c * 128:(c + 1) * 128], ("wv", c), eng="dve")
        X_PREISSUE = True
        p.op("pool", lambda e: e.memset(V[:, :, :, 128:129], 1.0), writes=["Vones"])

        def late_setup():
            for m in range(2):
                p.op("sp", (lambda e, m=m: e.dma_start(out=KTa[m][64:128, :], in_=zeros_d)), writes=[("KTpad", m)], dma=f"ldzk{m}")
                p.op("sp", (lambda e, m=m: e.dma_start(out=QTa[m][64:128, :], in_=zeros_d[:, 0:NOWN])), writes=[("QTpad", m)], dma=f"ldzq{m}")
                p.op("sp", (lambda e, m=m: e.dma_start(out=KTa[m][64:72, :], in_=kaug_d)), reads=[("KTpad", m)],
                     writes=[("KTaug", m)], dma=f"ldk{m}")
            for m in range(2):
                p.op("sp", (lambda e, m=m: e.dma_start(out=QTa[m][64:72, :], in_=qaug_d[0:8, :])),
                     reads=[("QTpad", m)], writes=[("QTaug", m)], dma=f"ldq{m}")

        def kq_tile(kind, tt, wsrc, wres_, bk):
            dstT = KTa if kind == "k" else QTa
            rname = "KT" if kind == "k" else "QT"
            for kc in range(KC):
                p.op("pe", (lambda e, kc=kc, bk=bk, tt=tt, wsrc=wsrc: e.matmul(
                    banks[bk][:], lhsT=wsrc[:, kc, :], rhs=hT_tile(kc, tt),
                    start=(kc == 0), stop=(kc == KC - 1))),
                     reads=[wres_, ("hT", tt)], writes=[("bank", bk)])
            p.op("dve", (lambda e, bk=bk, tt=tt, dstT=dstT: e.tensor_copy(out=dstT[0][0:64, tt * 512:(tt + 1) * 512],
                                                                       in_=banks[bk][0:64, :])),
                 reads=[("bank", bk)], writes=[(rname, 0)])
            p.op("act", (lambda e, bk=bk, tt=tt, dstT=dstT: e.copy(out=dstT[1][0:64, tt * 512:(tt + 1) * 512],
                                                                 in_=banks[bk][64:128, :])),
                 reads=[("bank", bk)], writes=[(rname, 1)])

        def vproj(i):
            bk = 4 + i % 2
            for kc in range(KC):
                p.op("pe", (lambda e, i=i, kc=kc, bk=bk: e.matmul(
                    banks[bk][:], lhsT=hT_blk(kc, i), rhs=wv_all[:, kc, :],
                    start=(kc == 0), stop=(kc == KC - 1))),
                     reads=[("hT", i // 4)] + [("wv", c) for c in range(4)], writes=[("bank", bk)])
            dstv = V[:, i, :, 0:128]
            srcv = (lambda bk=bk: banks[bk][:].rearrange("p (h e) -> p h e", h=4))
            if i % 2 == 0:
                p.op("act", (lambda e, dstv=dstv, srcv=srcv: e.copy(out=dstv, in_=srcv())), reads=[("bank", bk)], writes=[("V", i), "Vall"])
            else:
                p.op("dve", (lambda e, dstv=dstv, srcv=srcv: e.tensor_copy(out=dstv, in_=srcv())), reads=[("bank", bk)], writes=[("V", i), "Vall"])

        def follow(j):
            if 0 <= j < 32:
                vproj(j)
                if j % 4 == 3:
                    tt = j // 4
                    kq_tile("k", tt, wph0[5], ("wk0",), 6)
                    if tt < 4:
                        kq_tile("q", tt, wph0[4], ("wq0",), 7)

        NBLK = 33
        LAG = 6
        RAMP = {2: 0, 3: 1, 5: 2, 7: 3}
        RAMP_END = 10
        PRE = 2
        NXN = 3

        def stage_a(i):
            rows = 128 if i < 32 else 8
            xb_ = i % NXR
            nb = i % NXN
            p.op("act", (lambda e, xb_=xb_, rows=rows, i=i: e.activation(out=junk[0:rows, :], in_=xring[xb_][0:rows, :],
                                                                        func=AF.Square, accum_out=ss0[0:rows, i:i + 1])),
                 reads=[("x", xb_)], writes=[("ss0", i)])
            p.op("act", (lambda e, rows=rows, i=i: e.activation(out=sd0[0:rows, i:i + 1], in_=ss0[0:rows, i:i + 1], func=AF.Sqrt,
                                                                bias=S_EPSD[0:rows, :], scale=1.0 / D)),
                 reads=[("ss0", i), "epsd"], writes=[("sd0", i)])
            p.op("dve", (lambda e, rows=rows, i=i: e.reciprocal(out=rs0[0:rows, i:i + 1], in_=sd0[0:rows, i:i + 1])),
                 reads=[("sd0", i)], writes=[("rs0", i)])
            p.op("dve", (lambda e, rows=rows, i=i, xb_=xb_, nb=nb: e.tensor_scalar(out=xn[nb][0:rows, :], in0=xring[xb_][0:rows, :],
                                                                                   scalar1=rs0[0:rows, i:i + 1], scalar2=None, op0=ALU.mult)),
                 reads=[("x", xb_), ("rs0", i)], writes=[("xn", nb)])

        def x_load(i):
            if i >= NBLK:
                return
            rows = 128 if i < 32 else 8
            xb_ = i % NXR
            src = x_perm[i * 128:(i + 1) * 128, :] if i < 32 else x_halo
            p.op("sp", (lambda e, xb_=xb_, rows=rows, src=src: e.dma_start(out=xring[xb_][0:rows, :], in_=src)),
                 writes=[("x", xb_)], dma=f"ldx{xb_}")

        def stage_b(i):
            rows = 128 if i < 32 else 8
            nb = i % NXN
            bk = i % 4
            for kc in range(KC):
                p.op("pe", (lambda e, kc=kc, rows=rows, nb=nb, bk=bk: e.transpose(
                    bview_bf16(banks[bk])[:, kc * 128:kc * 128 + rows], xn[nb][0:rows, kc * 128:(kc + 1) * 128],
                    identb[0:rows, 0:rows])),
                     reads=[("xn", nb), "identb"], writes=[("bank", bk)])
            if i < 32:
                dst = hT2[i // 16][:, :, (i % 16) * 128:(i % 16 + 1) * 128]
                srcp = (lambda bk=bk: bview_bf16(banks[bk]).rearrange("p (a b) -> p a b", a=KC))
                wres = ("hT", i // 4)
            else:
                dst = hTh
                srcp = (lambda bk=bk: bview_bf16(banks[bk]).rearrange("p (a b) -> p a b", a=KC)[:, :, 0:8])
                wres = "hTh"
            if i % 2 == 1:
                p.op("act", (lambda e, dst=dst, srcp=srcp: e.copy(out=dst, in_=srcp())), reads=[("bank", bk)], writes=[wres])
            else:
                p.op("dve", (lambda e, dst=dst, srcp=srcp: e.tensor_copy(out=dst, in_=srcp())), reads=[("bank", bk)], writes=[wres])
            if i in RAMP:
                follow(RAMP[i])
            elif i >= RAMP_END:
                follow(i - LAG)

        deferred_casts = []
        for i in range(2, 5):
            x_load(i)
        load_w_chunk(8 + 3, wv_all[:, :, 3 * 128:4 * 128], ("wv", 3), eng="dve", xslot=5)
        load_w_chunk(4, wph0[5], ("wk0",), eng="dve", xslot=6, defer=deferred_casts)
        load_w_chunk(0, wph0[4], ("wq0",), eng="dve", xslot=7, defer=deferred_casts)
        x_load(5)
        for i in range(PRE):
            stage_a(i)
        for i in range(NBLK):
            if i + PRE < NBLK:
                stage_a(i + PRE)
            x_load(i + NXR)
            stage_b(i)
            if i == 2:
                for f_ in deferred_casts:
                    f_()
                x_load(6)
                x_load(7)
            if i == 5:
                late_setup()
        for j in range(NBLK - LAG, 32):
            follow(j)

        dbg(p, "hT", lambda: hT2[0][:, :, 0:512], [128, 8, 512], BF16, [("hT", 0)])
        dbg(p, "hTh", lambda: hTh, [128, 8, 8], BF16, ["hTh"])
        dbg(p, "V", lambda: V[:, 0:4, :, :], [128, 4, 4, 129], BF16, ["Vall", "Vones"])
        def sbank(m, buf):
            return banks[2 * buf + m]

        def sbank_id(m, buf):
            return 2 * buf + m

        def oacc(m, qb):
            a = m * 4 + qb
            return banks[4 + a // 3][:, (a % 3) * 129:(a % 3) * 129 + 129]

        def oacc_id(m, qb):
            return 4 + (m * 4 + qb) // 3

        ptctr = [0]
        def head_slots(h):
            if h % 2 == 1:
                return wbt[2], xn[1].rearrange("p (a b) -> p a b", a=KC), ("wq", 1), ("wk", 1), [], [("xn", 1)]
            return wbt[0], wbt[1], ("wq", 0), ("wk", 0), [], []

        def load_head_weights(h):
            wq, wk, wq_res, wk_res, extra_q, extra_k = head_slots(h)
            for c_, dst_, res_, ex_ in ((h, wq, wq_res, extra_q), (4 + h, wk, wk_res, extra_k)):
                s = wctr[0] % 3
                wctr[0] += 1
                p.op("sp", (lambda e, c_=c_, s=s: e.dma_start(out=wst[s], in_=w_in_r[c_].rearrange("p (a b) -> p a b", a=KC))),
                     writes=[("wst", s)], dma=f"ldw{s}")
                p.op("pool", (lambda e, s=s, dst_=dst_: e.tensor_tensor(out=dst_, in0=wst[s], in1=gb, op=ALU.mult)),
                     reads=[("wst", s), "gb"], writes=[res_] + ex_)

        W2_ORDER = [12, 13, 14, 15]
        for c_ in range(4):
            W2_ORDER += [20 + c_, 24 + c_, 16 + c_, 28 + c_]
        for j_ in range(KC):
            W2_ORDER += [32 + j_, 40 + j_]
        W2_SLOTS = [wbt[0], wbt[1], wbt[2], wb_extra[0], wb_extra[1]]
        w2_state = {"dma": 0, "cast": 0, "next": 0}
        W2_LOOK = 2

        W2_FIRST = {0: [("wq", 0)], 1: [("wk", 0)], 2: [("wq", 1)], 3: [("wst", 2)], 4: [("wst", 2)]}

        def w2_dma_until(n_last):
            while w2_state["dma"] <= min(n_last, len(W2_ORDER) - 1):
                n_ = w2_state["dma"]
                w2_state["dma"] += 1
                sg = n_ % 2
                p.op("sp", (lambda e, c=W2_ORDER[n_], sg=sg: e.dma_start(out=wst[sg], in_=w_in_r[c].rearrange("p (a b) -> p a b", a=KC))),
                     writes=[("wst", sg)], dma=f"ldw{sg}")

        def w2_cast_until(n_last):
            while w2_state["cast"] <= min(n_last, len(W2_ORDER) - 1):
                n_ = w2_state["cast"]
                w2_dma_until(n_ + 1)
                w2_state["cast"] += 1
                sg = n_ % 2
                sl = n_ % 5
                extra_w = W2_FIRST[sl] if n_ < 5 else []
                if n_ < 4:
                    p.op("dve", (lambda e, sg=sg, sl=sl: e.tensor_tensor(out=W2_SLOTS[sl], in0=wst[sg], in1=gb, op=ALU.mult)),
                         reads=[("wst", sg), "gb"], writes=[("wb2", sl)] + extra_w)
                    continue
                for kc in range(KC):
                    p.op("act", (lambda e, sg=sg, sl=sl, kc=kc: e.activation(out=W2_SLOTS[sl][:, kc, :], in_=wst[sg][:, kc, :],
                                                                              func=AF.Copy, scale=gb[:, kc, 0:1])),
                         reads=[("wst", sg), "gb"], writes=[("wb2", sl)] + extra_w)

        def w2_get():
            k = w2_state["next"]
            w2_state["next"] += 1
            w2_cast_until(k + W2_LOOK)
            return W2_SLOTS[k % 5], ("wb2", k % 5), W2_ORDER[k]

        def hn_items(hh):
            items = []

            def stats_():
                p.op("act", (lambda e: e.activation(out=sdo[:, hh * 16:(hh + 1) * 16], in_=sso[:, hh * 16:(hh + 1) * 16], func=AF.Sqrt,
                                                    bias=S_EPSO, scale=1.0 / 128)),
                     reads=[("sso", hh * 16 + b_) for b_ in range(16)] + ["epso"], writes=[("sdo", hh)])
                p.op("dve", (lambda e: e.reciprocal(out=sdo[:, hh * 16:(hh + 1) * 16], in_=sdo[:, hh * 16:(hh + 1) * 16])),
                     reads=[("sdo", hh)], writes=[("rso", hh)])
            items.append(stats_)
            for g in range(4):
                def scale_(g=g):
                    for q in range(4):
                        blk = 4 * g + q
                        col = hh * 16 + blk
                        p.op("dve", (lambda e, q=q, blk=blk, col=col: e.tensor_scalar(
                            out=ontmp[:, q, :], in0=oraw[:, hh, blk, :], scalar1=sdo[:, col:col + 1], scalar2=None, op0=ALU.mult)),
                             reads=[("rso", hh), ("oraw", hh, blk)], writes=[("ontmp", q)])

                def tr_(g=g):
                    for q in range(4):
                        p.op("pe", (lambda e, q=q: e.transpose(banks[7][:, q * 128:(q + 1) * 128], ontmp[:, q, :], identf)),
                             reads=[("ontmp", q), "identf"], writes=[("bank", 7)])
                    p.op("dve", (lambda e, g=g: e.tensor_scalar(out=oTn[hh][:, g * 512:(g + 1) * 512], in0=banks[7][:],
                                                               scalar1=S_GCOL, scalar2=None, op0=ALU.mult)),
                         reads=[("bank", 7), "gcol"], writes=[("oT", hh, g)] + [("oraw", hh, 4 * g + q_) for q_ in range(4)])
                items.append(scale_)
                items.append(tr_)
            return items

        load_head_weights(1)
        epi_ctr = [0]
        for h in range(4):
            hn_work = hn_items(h - 1) if h >= 1 else []
            if hn_work:
                hn_work.pop(0)()
            if h >= 1:
                wq, wk, wq_res, wk_res, _, _ = head_slots(h)
                for m in range(2):
                    p.op("sp", (lambda e, m=m, h=h: e.dma_start(out=QTa[m][64:72, :], in_=qaug_d[8 * h:8 * h + 8, :])),
                         reads=[("QTpad", m)], writes=[("QTaug", m)], dma=f"ldq{m}")
                for tt in range(8):
                    kq_tile("k", tt, wk, wk_res, tt % 4)
                for tt in range(4):
                    kq_tile("q", tt, wq, wq_res, tt % 4)
                if h + 1 < 4:
                    load_head_weights(h + 1)
            ucount = [0]
            for s_ in range(4):
                units = []
                for dl in range(4):
                    units.append((4 * s_ + dl, 128 * dl, True, False))
                for dl in range(4):
                    units.append((16 + 4 * s_ + dl, 0, False, True))
                for t in range(s_):
                    for dl in range(4):
                        units.append((4 * t + dl, 0, False, False))
                    for dl in range(4):
                        units.append((16 + 4 * t + dl, 0, False, False))
                nu = len(units)
                first_acc = [True]
                acc_started = {}

                def emit_qk(u):
                    j, c0, diag, xm = units[u]
                    buf = u % 2
                    for m in range(2):
                        bid = sbank_id(m, buf)
                        p.op("pe", (lambda e, m=m, j=j, c0=c0, buf=buf, diag=diag, s_=s_: e.matmul(
                            sbank(m, buf)[:, c0:512], lhsT=KTa[m][:, j * 128:(j + 1) * 128],
                            rhs=QTa[m][:, s_ * 512 + c0:(s_ + 1) * 512], start=True, stop=(not diag))),
                             reads=[("KT", m), ("KTaug", m), ("KTpad", m), ("QT", m), ("QTaug", m), ("QTpad", m)],
                             writes=[("bank", bid)])
                        if diag:
                            p.op("pe", (lambda e, m=m, c0=c0, buf=buf: e.matmul(
                                sbank(m, buf)[:, c0:c0 + 128], lhsT=identb, rhs=tri, start=False, stop=True)),
                                 reads=["identb", "tri"], writes=[("bank", bid)])

                def emit_exp(u):
                    j, c0, diag, xm = units[u]
                    buf = u % 2
                    res = []
                    for m in range(2):
                        bid = sbank_id(m, buf)
                        pi = ptctr[0] % NPT
                        res.append(pi)
                        p.op("act", (lambda e, m=m, buf=buf, c0=c0, pi=pi: e.activation(
                            out=PT[m][pi][:, c0:512], in_=sbank(m, buf)[:, c0:512], func=AF.Exp, scale=0.125)),
                             reads=[("bank", bid)], writes=[("PT", m, pi)])
                    ptctr[0] += 1
                    return res[0]

                def emit_av(u, pi):
                    j, c0, diag, xm = units[u]
                    for m in range(2):
                        for qb in range(c0 // 128, 4):
                            bid = oacc_id(m, qb)
                            st_flag = bid not in acc_started
                            acc_started[bid] = True
                            p.op("pe", (lambda e, m=m, qb=qb, j=j, pi=pi, st_flag=st_flag, h=h, lastu=(u == nu - 1): e.matmul(
                                oacc(m, qb), lhsT=PT[m][pi][:, qb * 128:(qb + 1) * 128], rhs=V[:, j, h, :],
                                start=st_flag, stop=lastu, skip_group_check=True)),
                                 reads=[("PT", m, pi), "Vall", "Vones"], writes=[("bank", bid)])

                pis = {}
                emit_qk(0)
                for u in range(nu):
                    pis[u] = emit_exp(u)
                    if u + 1 < nu:
                        emit_qk(u + 1)
                    emit_av(u, pis[u])
                    ucount[0] += 1
                    if hn_work and ucount[0] >= 8 and ucount[0] % 6 == 2:
                        hn_work.pop(0)()
                    if h == 3:
                        if ucount[0] == 5:
                            w2_dma_until(1)
                        elif ucount[0] in (15, 25, 35, 45):
                            w2_cast_until((ucount[0] - 15) // 10)

                ob = epi_ctr[0] % 2
                first = epi_ctr[0] < 2
                epi_ctr[0] += 1
                for bnk, a0, a1 in ((4, 0, 3), (5, 3, 6), (6, 6, 8)):
                    n_ = a1 - a0
                    p.op("dve", (lambda e, bnk=bnk, a0=a0, a1=a1, n_=n_, ob=ob: e.tensor_copy(
                        out=ocp[ob][:, a0:a1, :], in_=banks[bnk][:, 0:n_ * 129].rearrange("p (a e) -> p a e", a=n_))),
                         reads=[("bank", bnk)],
                         writes=[("ocp", ob, bnk)] + ([("wv", c) for c in range(4)] + [("wq0",), ("wk0",)] if first else []))
                ocp_res = [("ocp", ob, 4), ("ocp", ob, 5), ("ocp", ob, 6)]
                rl = S_R[:, 0:8]
                p.op("dve", (lambda e, ob=ob, rl=rl: e.reciprocal(out=rl.rearrange("p (a o) -> p a o", o=1), in_=ocp[ob][:, :, 128:129])),
                     reads=ocp_res, writes=["rl"])
                p.op("dve", (lambda e: e.tensor_scalar(out=S_R[:, 8:12], in0=S_R[:, 4:8], scalar1=S_NLAM, scalar2=None, op0=ALU.mult)),
                     reads=["rl", "nlam"], writes=["r1n"])
                for qb in range(4):
                    blk = s_ * 4 + qb
                    col = h * 16 + blk
                    p.op("dve", (lambda e, qb=qb, ob=ob: e.tensor_scalar(out=att_t[:, 0:128], in0=ocp[ob][:, qb, 0:128], scalar1=S_R[:, qb:qb + 1],
                                                                        scalar2=None, op0=ALU.mult)),
                         reads=ocp_res + ["rl"], writes=["att_t"])
                    xw = [("x", k_) for k_ in range(NXR)] if (h == 0 and s_ == 0) else []
                    p.op("dve", (lambda e, qb=qb, blk=blk, h=h, ob=ob: e.scalar_tensor_tensor(
                        out=oraw[:, h, blk, :], in0=ocp[ob][:, 4 + qb, 0:128], scalar=S_R[:, 8 + qb:9 + qb], in1=att_t[:, 0:128],
                        op0=ALU.mult, op1=ALU.add)),
                         reads=ocp_res + ["r1n", "att_t"], writes=[("oraw", h, blk)] + xw)
                    p.op("dve", (lambda e, blk=blk, col=col, h=h: e.scalar_tensor_tensor(
                        out=att_t[:, 128:256], in0=oraw[:, h, blk, :], scalar=1.0, in1=oraw[:, h, blk, :],
                        op0=ALU.mult, op1=ALU.mult, accum_out=sso[:, col:col + 1])),
                         reads=[("oraw", h, blk)], writes=[("sso", col), "att_t2"])

            while hn_work:
                hn_work.pop(0)()

        for m in range(2):
            dbg(p, f"KTa{m}", (lambda m=m: KTa[m]), [128, NT], BF16, [("KT", m), ("KTaug", m), ("KTpad", m)])
            dbg(p, f"QTa{m}", (lambda m=m: QTa[m]), [128, NOWN], BF16, [("QT", m), ("QTaug", m), ("QTpad", m)])
        dbg(p, "oraw", lambda: oraw, [128, 4, 16, 128], F32, [("oraw", h, b) for h in range(4) for b in range(16)])
        dbg(p, "small", lambda: small, [128, 256], F32, ["nlam", "gcol"])
        for f_ in hn_items(3):
            f_()
        bctr = [0]

        def next_bank():
            b = bctr[0] % 8
            bctr[0] += 1
            return b

        tctr2 = [0]

        def next_tmp():
            t = tctr2[0] % 8
            tctr2[0] += 1
            return t

        def proj_tile(wsrc, wres_, tt, bk):
            for kc in range(KC):
                p.op("pe", (lambda e, kc=kc, bk=bk, tt=tt, wsrc=wsrc: e.matmul(
                    banks[bk][:], lhsT=wsrc[:, kc, :], rhs=hT_tile(kc, tt),
                    start=(kc == 0), stop=(kc == KC - 1))),
                     reads=[wres_, ("hT", tt)], writes=[("bank", bk)])

        on_list = [(h_, g_) for h_ in range(4) for g_ in range(4)]
        NPRE = 7
        za_w = {}
        pre_bank = {}
        for idx in range(NPRE):
            c, tt = on_list[idx]
            if tt == 0:
                za_w[c] = w2_get()
                assert za_w[c][2] == 12 + c
            pre_bank[idx] = next_bank()
            proj_tile(za_w[c][0], za_w[c][1], tt, pre_bank[idx])
        p.barrier()
        p.op("pool", lambda e: e.dma_start(out=wao, in_=wao_r.rearrange("p (c n) -> p c n", c=4)), writes=["wao"], dma="ldpa")
        p.op("pool", lambda e: e.dma_start(out=wco, in_=wco_r.rearrange("p (c n) -> p c n", c=4)), writes=["wco"], dma="ldpc")
        ld("sp", gpost, gpost_d.partition_broadcast(128), "gpost", "ldg")
        for idx, (c, tt) in enumerate(on_list):
            if idx >= NPRE and tt == 0:
                za_w[c] = w2_get()
                assert za_w[c][2] == 12 + c
            if idx in pre_bank:
                bk = pre_bank[idx]
            else:
                bk = next_bank()
                proj_tile(za_w[c][0], za_w[c][1], tt, bk)
            ti = next_tmp()
            p.op("act", (lambda e, bk=bk, ti=ti: e.activation(out=tmp[ti], in_=banks[bk][:], func=AF.Silu)),
                 reads=[("bank", bk)], writes=[("tmp", ti)])
            p.op("dve", (lambda e, c=c, tt=tt, ti=ti: e.tensor_tensor(out=gattnT[:, c, tt * 512:(tt + 1) * 512], in0=tmp[ti],
                                                                     in1=oTn[c][:, tt * 512:(tt + 1) * 512], op=ALU.mult)),
                 reads=[("tmp", ti), ("oT", c, tt)], writes=[("gattnT", c, tt)])

        pr = prod[0]
        for c in range(4):
            wcb, wcc, wcx, wzc = None, None, None, None
            W_ws_cc, R_ws_cc, cid_ = w2_get()
            assert cid_ == 20 + c
            W_ws_cx, R_ws_cx, cid_ = w2_get()
            assert cid_ == 24 + c
            bkh = next_bank()
            for kc in range(KC):
                p.op("pe", (lambda e, kc=kc, bkh=bkh, W_ws_cc=W_ws_cc: e.matmul(
                    banks[bkh][:, 0:8], lhsT=W_ws_cc[:, kc, :], rhs=hTh[:, kc, :], start=(kc == 0), stop=(kc == KC - 1),
                    skip_group_check=True)),
                     reads=[R_ws_cc, "hTh"], writes=[("bank", bkh)])
            for kc in range(KC):
                p.op("pe", (lambda e, kc=kc, bkh=bkh, W_ws_cx=W_ws_cx: e.matmul(
                    banks[bkh][:, 8:16], lhsT=W_ws_cx[:, kc, :], rhs=hTh[:, kc, :], start=False, stop=(kc == KC - 1),
                    skip_group_check=True)),
                     reads=[R_ws_cx, "hTh"], writes=[("bank", bkh)])
            tih = next_tmp()
            p.op("act", (lambda e, bkh=bkh, tih=tih: e.copy(out=tmp[tih][:, 0:8], in_=banks[bkh][:, 0:8])),
                 reads=[("bank", bkh)], writes=[("tmp", tih)])
            p.op("dve", (lambda e, bkh=bkh, tih=tih: e.tensor_tensor(
                out=pr[:, :, 0:2], in0=banks[bkh][:, 8:16].rearrange("p (t n) -> p t n", t=4),
                in1=tmp[tih][:, 0:8].rearrange("p (t n) -> p t n", t=4), op=ALU.mult)),
                 reads=[("bank", bkh), ("tmp", tih)], writes=["prod_h"])
            for tt in range(4):
                bk1 = next_bank()
                proj_tile(W_ws_cc, R_ws_cc, tt, bk1)
                bk2 = next_bank()
                proj_tile(W_ws_cx, R_ws_cx, tt, bk2)
                ti = next_tmp()
                p.op("act", (lambda e, bk1=bk1, ti=ti: e.copy(out=tmp[ti], in_=banks[bk1][:])),
                     reads=[("bank", bk1)], writes=[("tmp", ti)])
                p.op("dve", (lambda e, bk2=bk2, ti=ti, tt=tt: e.tensor_tensor(out=pr[:, tt, 2:514], in0=banks[bk2][:], in1=tmp[ti],
                                                                             op=ALU.mult)),
                     reads=[("bank", bk2), ("tmp", ti)], writes=[("prod", tt)])
            W_ws_cb, R_ws_cb, cid_ = w2_get()
            assert cid_ == 16 + c
            utiles = []
            for tt in range(4):
                tu = next_tmp()
                utiles.append(tu)
                p.op("dve", (lambda e, tt=tt, tu=tu, c=c: e.tensor_scalar(out=tmp[tu], in0=pr[:, tt, 0:512], scalar1=cw[:, 3 * c:3 * c + 1],
                                                                         scalar2=None, op0=ALU.mult)),
                     reads=[("prod", tt), "prod_h", "cw"], writes=[("tmp", tu)])
                p.op("dve", (lambda e, tt=tt, tu=tu, c=c: e.scalar_tensor_tensor(out=tmp[tu], in0=pr[:, tt, 1:513],
                                                                                scalar=cw[:, 3 * c + 1:3 * c + 2], in1=tmp[tu],
                                                                                op0=ALU.mult, op1=ALU.add)),
                     reads=[("prod", tt), "prod_h", "cw", ("tmp", tu)], writes=[("tmp", tu)])
                p.op("dve", (lambda e, tt=tt, tu=tu, c=c: e.scalar_tensor_tensor(out=tmp[tu], in0=pr[:, tt, 2:514],
                                                                                scalar=cw[:, 3 * c + 2:3 * c + 3], in1=tmp[tu],
                                                                                op0=ALU.mult, op1=ALU.add)),
                     reads=[("prod", tt), "cw", ("tmp", tu)], writes=[("tmp", tu)])
                bk = next_bank()
                proj_tile(W_ws_cb, R_ws_cb, tt, bk)
                p.op("dve", (lambda e, bk=bk, tu=tu: e.tensor_tensor(out=tmp[tu], in0=banks[bk][:], in1=tmp[tu], op=ALU.mult)),
                     reads=[("bank", bk), ("tmp", tu)], writes=[("tmp", tu)])
            W_ws_zc, R_ws_zc, cid_ = w2_get()
            assert cid_ == 28 + c
            for tt in range(4):
                tu = utiles[tt]
                bk = next_bank()
                proj_tile(W_ws_zc, R_ws_zc, tt, bk)
                ti = next_tmp()
                while ti in utiles:
                    ti = next_tmp()
                p.op("act", (lambda e, bk=bk, ti=ti: e.activation(out=tmp[ti], in_=banks[bk][:], func=AF.Silu)),
                     reads=[("bank", bk)], writes=[("tmp", ti)])
                p.op("dve", (lambda e, c=c, tt=tt, ti=ti, tu=tu: e.tensor_tensor(out=gconvT[:, c, tt * 512:(tt + 1) * 512], in0=tmp[tu],
                                                                                in1=tmp[ti], op=ALU.mult)),
                     reads=[("tmp", ti), ("tmp", tu)], writes=[("gconvT", c, tt)])

        oT_all = [("oT", h, g) for h in range(4) for g in range(4)]
        p.op("pool", lambda e: e.dma_start(out=wo, in_=wo_r.rearrange("p (j n) -> p j n", j=KC)), writes=["wo"] + oT_all, dma="ldpo")
        for j in range(KC):
            W_ws_a, R_ws_a, cid_ = w2_get()
            assert cid_ == 32 + j
            W_ws_c, R_ws_c, cid_ = w2_get()
            assert cid_ == 40 + j
            for tt in range(4):
                bka = next_bank()
                proj_tile(W_ws_a, R_ws_a, tt, bka)
                bkc = next_bank()
                proj_tile(W_ws_c, R_ws_c, tt, bkc)
                bya = next_bank()
                for c in range(4):
                    p.op("pe", (lambda e, c=c, j=j, tt=tt, bya=bya: e.matmul(
                        banks[bya][:], lhsT=wao[:, c, j * 128:(j + 1) * 128], rhs=gattnT[:, c, tt * 512:(tt + 1) * 512],
                        start=(c == 0), stop=(c == 3))),
                         reads=["wao", ("gattnT", c, tt)], writes=[("bank", bya)])
                byc = next_bank()
                for c in range(4):
                    p.op("pe", (lambda e, c=c, j=j, tt=tt, byc=byc: e.matmul(
                        banks[byc][:], lhsT=wco[:, c, j * 128:(j + 1) * 128], rhs=gconvT[:, c, tt * 512:(tt + 1) * 512],
                        start=(c == 0), stop=(c == 3))),
                         reads=["wco", ("gconvT", c, tt)], writes=[("bank", byc)])
                ta = next_tmp()
                tc = next_tmp()
                p.op("act", (lambda e, bka=bka, ta=ta, j=j: e.activation(out=tmp[ta], in_=banks[bka][:], func=AF.Sigmoid,
                                                                        bias=bm[:, j:j + 1])),
                     reads=[("bank", bka), "bm"], writes=[("tmp", ta)])
                p.op("act", (lambda e, bkc=bkc, tc=tc, j=j: e.activation(out=tmp[tc], in_=banks[bkc][:], func=AF.Sigmoid,
                                                                        bias=bm[:, 8 + j:9 + j])),
                     reads=[("bank", bkc), "bm"], writes=[("tmp", tc)])
                p.op("dve", (lambda e, bya=bya, ta=ta: e.tensor_tensor(out=tmp[ta], in0=banks[bya][:], in1=tmp[ta], op=ALU.mult)),
                     reads=[("bank", bya), ("tmp", ta)], writes=[("tmp", ta)])
                p.op("dve", (lambda e, byc=byc, tc=tc: e.tensor_tensor(out=tmp[tc], in0=banks[byc][:], in1=tmp[tc], op=ALU.mult)),
                     reads=[("bank", byc), ("tmp", tc)], writes=[("tmp", tc)])
                p.op("pool", (lambda e, ta=ta, tc=tc, j=j, tt=tt: e.tensor_tensor(out=yT[:, j, tt * 512:(tt + 1) * 512], in0=tmp[ta],
                                                                                 in1=tmp[tc], op=ALU.add)),
                     reads=[("tmp", ta), ("tmp", tc)],
                     writes=[("yT", tt)] + ([("oraw", h_, b_) for h_ in range(4) for b_ in range(16)]
                                            + [("oT", h_, g_) for h_ in range(4) for g_ in range(4)] if j == 0 else []))

        dbg(p, "gattnT", lambda: gattnT, [128, 4, 2048], BF16, [("gattnT", c, t) for c in range(4) for t in range(4)])
        dbg(p, "gconvT", lambda: gconvT, [128, 4, 2048], BF16, [("gconvT", c, t) for c in range(4) for t in range(4)])
        outs = []
        g_all = [(nm_, c_, t_) for nm_ in ("gattnT", "gconvT") for c_ in range(4) for t_ in range(4)]
        def xres_load(i):
            if i >= 16:
                return
            sb_ = i % NOS
            p.op("sp", (lambda e, i=i, sb_=sb_: e.dma_start(out=xres[sb_], in_=x_perm[i * 128:(i + 1) * 128, :])),
                 writes=[("xres", sb_)] + (g_all if i < NOS else []), dma=f"ldr{sb_}")

        for i in range(NOS - 1):
            xres_load(i)
        for i in range(16):
            sb_ = i % NOS
            xres_load(i + NOS - 1)
            bh = [next_bank(), next_bank()]
            for half in range(2):
                for j in range(KC):
                    p.op("pe", (lambda e, i=i, j=j, half=half, bh=bh: e.matmul(
                        banks[bh[half]][:], lhsT=yT[:, j, i * 128:(i + 1) * 128], rhs=wo[:, j, half * 512:(half + 1) * 512],
                        start=(j == 0), stop=(j == KC - 1))),
                         reads=[("yT", i // 4), "wo"], writes=[("bank", bh[half])])
            for half in range(2):
                p.op("act", (lambda e, half=half, bh=bh, i=i: e.activation(out=junk[:, half * 512:(half + 1) * 512], in_=banks[bh[half]][:],
                                                                          func=AF.Square, accum_out=ss0[:, 2 * i + half:2 * i + half + 1])),
                     reads=[("bank", bh[half])], writes=[("ssf", i, half)])
            p.op("dve", (lambda e, i=i: e.tensor_tensor(out=sd0[:, i:i + 1], in0=ss0[:, 2 * i:2 * i + 1], in1=ss0[:, 2 * i + 1:2 * i + 2],
                                                        op=ALU.add)),
                 reads=[("ssf", i, 0), ("ssf", i, 1)], writes=[("sdf", i)])
            p.op("act", (lambda e, i=i: e.activation(out=sd0[:, i:i + 1], in_=sd0[:, i:i + 1], func=AF.Sqrt, bias=S_EPSD, scale=1.0 / D)),
                 reads=[("sdf", i), "epsd"], writes=[("sdf2", i)])
            p.op("dve", (lambda e, i=i: e.reciprocal(out=rs0[:, i:i + 1], in_=sd0[:, i:i + 1])), reads=[("sdf2", i)], writes=[("rsf", i)])
            for half in range(2):
                p.op("dve", (lambda e, half=half, bh=bh, i=i, sb_=sb_: e.scalar_tensor_tensor(
                    out=ostage[sb_][:, half * 512:(half + 1) * 512], in0=banks[bh[half]][:], scalar=rs0[:, i:i + 1],
                    in1=gpost[:, half * 512:(half + 1) * 512], op0=ALU.mult, op1=ALU.mult)),
                     reads=[("bank", bh[half]), ("rsf", i), "gpost"], writes=[("ost", sb_, half)] + (g_all if i < NOS else []))
            p.op("pool", (lambda e, sb_=sb_: e.tensor_tensor(out=ostage[sb_], in0=ostage[sb_], in1=xres[sb_], op=ALU.add)),
                 reads=[("ost", sb_, 0), ("ost", sb_, 1), ("xres", sb_)], writes=[("ostf", sb_)])
            outs.append(p.op("sp", (lambda e, i=i, sb_=sb_: e.dma_start(out=out_own[i * 128:(i + 1) * 128, :], in_=ostage[sb_])),
                             reads=[("ostf", sb_)], writes=[("ost", sb_, 0), ("ost", sb_, 1)], dma=f"st{sb_}"))
        dbg(p, "yT", lambda: yT, [128, 8, 2048], BF16, [("yT", t) for t in range(4)])
        p.run(final_ops=outs[-NOS:] + dbg_outs)
    return nc


_NC_CACHE = {}


def _get_program(debug=False):
    key = "nc_dbg" if debug else "nc"
    if key not in _NC_CACHE:
        _NC_CACHE[key] = build_program(debug)
    return _NC_CACHE[key]


def _core_layout(r):
    own = [2 * s + r for s in range(4)]
    oth = [2 * s + 1 - r for s in range(4)]
    tiles = own + oth
    pos = np.concatenate([np.arange(512 * t, 512 * t + 512) for t in tiles])
    halo = []
    for t in own:
        for dlt in (2, 1):
            halo.append(512 * t - dlt)
    return own, pos, np.array(halo)


def kernel(x, w_in, lambda_q1, lambda_k1, lambda_q2, lambda_k2, subln_gain, conv_w,
           w_attn_o, w_conv_o, b_merge, w_out, g_pre, g_post, _debug=False):
    x = np.asarray(x, dtype=np.float32)
    f32 = lambda a: np.ascontiguousarray(np.asarray(a, dtype=np.float32))
    w_in0 = f32(w_in)[0]
    w_in_r = np.ascontiguousarray(w_in0.reshape(KC, 128, 48, 128).transpose(2, 1, 0, 3).reshape(48, 128, KC * 128))
    wao_r = np.ascontiguousarray(f32(w_attn_o)[0].reshape(4, 128, D).transpose(1, 0, 2).reshape(128, 4 * D))
    wco_r = np.ascontiguousarray(f32(w_conv_o)[0].reshape(4, 128, D).transpose(1, 0, 2).reshape(128, 4 * D))
    wo_r = np.ascontiguousarray(f32(w_out)[0].reshape(8, 128, D).transpose(1, 0, 2).reshape(128, 8 * D))
    lamv = np.concatenate([f32(lambda_q1)[0], f32(lambda_k1)[0], f32(lambda_q2)[0], f32(lambda_k2)[0]]).reshape(1, 256)
    gsub = f32(subln_gain)[0].reshape(128, 1)
    cw = np.ascontiguousarray(f32(conv_w)[0].reshape(3, 4, 128).transpose(2, 1, 0).reshape(128, 12))
    bm = np.ascontiguousarray(f32(b_merge)[0].reshape(16, 128).T)
    gb = np.ascontiguousarray(np.repeat(f32(g_pre)[0].reshape(8, 128).T[:, :, None], 128, axis=2).reshape(128, KC * 128))
    gpost = f32(g_post)[0].reshape(1, D)
    identf = np.eye(128, dtype=np.float32)
    identb = identf.astype(ml_dtypes.bfloat16)
    kk = np.arange(128)
    tri = np.where(kk[:, None] <= kk[None, :], 0.0, NEG).astype(np.float32).astype(ml_dtypes.bfloat16)

    zeros_bf = np.zeros((64, NT), dtype=ml_dtypes.bfloat16)
    in_maps = []
    layouts = []
    for c in range(8):
        b, r = c // 2, c % 2
        own, pos, halo = _core_layout(r)
        layouts.append((b, own))
        x_perm = np.ascontiguousarray(x[b][pos])
        x_halo = np.zeros((8, D), dtype=np.float32)
        for i, hp in enumerate(halo):
            if hp >= 0:
                x_halo[i] = x[b][hp]
        kidx = np.arange(NT)
        xind = [((kidx >= NOWN + 512 * t) & (kidx < NOWN + 512 * (t + 1))).astype(np.float32) for t in range(4)]
        kaug = np.stack([pos // 128, pos % 128, np.ones_like(pos), np.ones_like(pos)] + xind).astype(np.float32)
        qpos = pos[:NOWN]
        qa = []
        for h in range(4):
            sl = SLOPES[h]
            xm_rows = [np.where((np.arange(NOWN) // 512 == t) & (r == 0), NEG, 0.0) for t in range(4)]
            qa.append(np.stack([np.full(NOWN, 8 * sl * 128), np.full(NOWN, 8 * sl),
                                -8 * sl * 128 * (qpos // 128), -8 * sl * (qpos % 128)] + xm_rows).astype(np.float32))
        qaug = np.concatenate(qa, axis=0)
        xbias = np.full((128, 1), 0.0 if r == 1 else -30000.0, dtype=np.float32)
        in_maps.append(dict(
            x_perm=x_perm, x_halo=x_halo, w_in_r=w_in_r, wao_r=wao_r, wco_r=wco_r, wo_r=wo_r, lamv=lamv, gsub=gsub,
            cw=cw, bm=bm, gb=gb, gpost=gpost, identb=identb, identf=identf, tri=tri,
            kaug=kaug.astype(ml_dtypes.bfloat16), qaug=qaug.astype(ml_dtypes.bfloat16), xbias=xbias, zeros=zeros_bf))

    nc = _get_program(_debug)
    res = run_bass_kernel_spmd(nc, in_maps, core_ids=list(range(8)))
    if _debug:
        _NC_CACHE["last_results"] = res.results
    out = np.empty_like(x)
    for c in range(8):
        b, own = layouts[c]
        oo = res.results[c]["out_own"]
        for s, t in enumerate(own):
            out[b, 512 * t:512 * t + 512] = oo[512 * s:512 * s + 512]
    return out
```

```python
import contextlib
import math

import ml_dtypes
import numpy as np

import concourse.bass as bass
import concourse.mybir as mybir
from concourse.bass_utils import run_bass_kernel_spmd

F32 = mybir.dt.float32
BF16 = mybir.dt.bfloat16
AF = mybir.ActivationFunctionType
ALU = mybir.AluOpType

ENGS = ("sp", "act", "pe", "dve", "pool")

D = 1024
KC = 8
NT = 4096
NOWN = 2048
NEG = -262144.0
EPS = 1e-6
LAM_INIT = 0.8 - 0.6 * math.exp(-0.3 * 0)
SLOPES = [2.0 ** (-8.0 * (i + 1) / 4) for i in range(4)]


class Op:
    __slots__ = ("eng", "fn", "waits", "token", "needs_inc", "dma", "idx")

    def __init__(self, eng, fn, dma):
        self.eng = eng
        self.fn = fn
        self.waits = []
        self.needs_inc = False
        self.dma = dma
        self.token = None


class Prog:
    def __init__(self, nc, stack, same_engine_sync=("act", "dve", "pool")):
        self.nc = nc
        self.stack = stack
        self.ops = {e: [] for e in ENGS}
        self.sems = {}
        self.last_w = {}
        self.readers = {}
        self.dma_count = {}
        self.same_engine_sync = set(same_engine_sync)
        self.pending_barrier = {}

    def op(self, eng, fn, reads=(), writes=(), dma=None, extra=()):
        o = Op(eng, fn, dma)
        o.idx = len(self.ops[eng])
        deps = list(extra)
        if eng in self.pending_barrier:
            deps.extend(self.pending_barrier.pop(eng))
        for r in reads:
            w = self.last_w.get(r)
            if w is not None:
                deps.append(w)
        for r in writes:
            w = self.last_w.get(r)
            if w is not None:
                deps.append(w)
            deps.extend(self.readers.get(r, ()))
        latest = {}
        seen = set()
        for d in deps:
            if d is o or id(d) in seen:
                continue
            seen.add(id(d))
            if d.dma is None:
                if d.eng == eng and eng not in self.same_engine_sync:
                    continue
                if d.eng not in latest or d.idx > latest[d.eng].idx:
                    latest[d.eng] = d
            else:
                o.waits.append(d)
                d.needs_inc = True
        for d in latest.values():
            o.waits.append(d)
            d.needs_inc = True
        for r in reads:
            self.readers.setdefault(r, []).append(o)
        for r in writes:
            self.last_w[r] = o
            self.readers[r] = []
        if dma is not None:
            o.needs_inc = True
        self.ops[eng].append(o)
        return o

    def barrier(self):
        lasts = []
        for e in ENGS:
            if self.ops[e]:
                lasts.append(self.ops[e][-1])
            seen_streams = set()
            for o in reversed(self.ops[e]):
                if o.dma is not None and o.dma not in seen_streams:
                    seen_streams.add(o.dma)
                    lasts.append(o)
        for e in ENGS:
            self.pending_barrier[e] = list(lasts)

    def finalize(self):
        totals = {}
        for e in ENGS:
            for o in self.ops[e]:
                if o.dma is not None:
                    totals[o.dma] = totals.get(o.dma, 0) + 1
        for e in ENGS:
            c = 0
            for o in self.ops[e]:
                if o.dma is not None:
                    k = self.dma_count.get(o.dma, 0) + 1
                    self.dma_count[o.dma] = k
                    if o.dma.startswith("G:"):
                        k = totals[o.dma]
                    o.token = ("d_" + o.dma.replace(":", "_"), 16 * k)
                elif o.needs_inc:
                    c += 1
                    o.token = ("e_" + e, c)
        names = sorted(set(o.token[0] for e in ENGS for o in self.ops[e] if o.token))
        for n in names:
            self.sems[n] = self.stack.enter_context(self.nc.semaphore("s_" + n))

    def replay(self, eng_name, engine):
        waited = {}
        for o in self.ops[eng_name]:
            need = {}
            for d in o.waits:
                s, v = d.token
                if v > need.get(s, 0):
                    need[s] = v
            for s in sorted(need):
                v = need[s]
                if waited.get(s, 0) >= v:
                    continue
                engine.wait_ge(self.sems[s], v)
                waited[s] = v
            ins = o.fn(engine)
            if o.token is not None:
                ins.then_inc(self.sems[o.token[0]], 16 if o.dma is not None else 1)

    def run(self, final_ops=()):
        self.finalize()
        with self.nc.Block() as block:
            @block.sync
            def _(e):
                self.replay("sp", e)
                for o in final_ops:
                    e.wait_ge(self.sems[o.token[0]], o.token[1])

            @block.scalar
            def _(e):
                self.replay("act", e)

            @block.tensor
            def _(e):
                self.replay("pe", e)

            @block.vector
            def _(e):
                self.replay("dve", e)

            @block.gpsimd
            def _(e):
                self.replay("pool", e)


def build_program(debug=False):
    nc = bass.Bass("TRN2", target_bir_lowering=False)
    dbg_outs = []

    def dbg(p, name, ap_fn, shape, dt, reads):
        if not debug:
            return
        t = nc.dram_tensor("dbg_" + name, list(shape), dt, kind="ExternalOutput").ap()
        dbg_outs.append(p.op("sp", (lambda e: e.dma_start(out=t, in_=ap_fn())), reads=reads, dma=f"dbg{len(dbg_outs)}"))

    def din(name, shape, dt=F32):
        return nc.dram_tensor(name, list(shape), dt, kind="ExternalInput").ap()

    x_perm = din("x_perm", [NT, D])
    x_halo = din("x_halo", [8, D])
    w_in_r = din("w_in_r", [48, 128, KC * 128])
    wao_r = din("wao_r", [128, 4 * D])
    wco_r = din("wco_r", [128, 4 * D])
    wo_r = din("wo_r", [128, 8 * D])
    lamv = din("lamv", [1, 256])
    gsub = din("gsub", [128, 1])
    cw_d = din("cw", [128, 12])
    bm_d = din("bm", [128, 16])
    gb_d = din("gb", [128, KC * 128])
    gpost_d = din("gpost", [1, D])
    identb_d = din("identb", [128, 128], BF16)
    identf_d = din("identf", [128, 128])
    tri_d = din("tri", [128, 128], BF16)
    kaug_d = din("kaug", [8, NT], BF16)
    qaug_d = din("qaug", [32, NOWN], BF16)
    xbias_d = din("xbias", [128, 1])
    zeros_d = din("zeros", [64, NT], BF16)
    out_own = nc.dram_tensor("out_own", [NOWN, D], F32, kind="ExternalOutput").ap()

    with contextlib.ExitStack() as st:
        K = 1024
        ARENA_BYTES = 212736
        arena = st.enter_context(nc.sbuf_tensor("arena", [128, ARENA_BYTES // 2], BF16))
        banks = [st.enter_context(nc.psum_tensor(f"bank{i}", [128, 512], F32)) for i in range(8)]

        def reg(off, nbytes, dt):
            a = arena[:, off // 2:(off + nbytes) // 2]
            return a.bitcast(F32) if dt == F32 else a

        o = 0

        def take(n):
            nonlocal o
            r = o
            o += (n + 63) // 64 * 64
            return r

        identb = reg(take(256), 256, BF16)
        tri = reg(take(256), 256, BF16)
        identf = reg(take(512), 512, F32)
        lamt = reg(take(1024), 1024, F32)
        small = reg(take(1024), 1024, F32)
        cw = reg(take(48), 48, F32)
        bm = reg(take(64), 64, F32)
        gpre = reg(take(32), 32, F32)
        stats = reg(take(1024), 1024, F32)
        gb = reg(take(4096), 4096, F32).rearrange("p (a b) -> p a b", a=KC)
        assert o <= 9 * K, o
        o = 9 * K
        HT0 = o
        hT2 = [reg(take(32 * K), 32 * K, BF16).rearrange("p (a b) -> p a b", a=KC) for _ in range(2)]

        def hT_tile(kc, tt):
            return hT2[tt // 4][:, kc, (tt % 4) * 512:(tt % 4 + 1) * 512]

        def hT_blk(kc, i):
            return hT2[i // 16][:, kc, (i % 16) * 128:(i % 16 + 1) * 128]
        hTh = reg(take(128), 128, BF16).rearrange("p (a b) -> p a b", a=KC)
        base_ph = o
        V = reg(take(33024), 33024, BF16).rearrange("p (j h e) -> p j h e", j=32, h=4)
        kq_off = o
        KTa = [reg(take(8 * K), 8 * K, BF16) for _ in range(2)]
        QTa = [reg(take(4 * K), 4 * K, BF16) for _ in range(2)]
        NPT = 3
        PT = [[reg(take(K), K, BF16) for _ in range(NPT)] for _ in range(2)]
        oraw_off = take(32 * K)
        oraw = reg(oraw_off, 32 * K, F32).rearrange("p (h b e) -> p h b e", h=4, b=16)
        NXR = 8
        xring = [reg(oraw_off + i * 4 * K, 4 * K, F32) for i in range(NXR)]
        oTn = [reg(oraw_off + h_ * 8 * K, 8 * K, F32) for h_ in range(4)]
        ph01_tail = o
        wph0 = [reg(take(2 * K), 2 * K, BF16).rearrange("p (a b) -> p a b", a=KC) for _ in range(6)]
        ocp = [reg(ph01_tail + b_ * 4160, 4128, F32).rearrange("p (a e) -> p a e", a=8) for b_ in range(2)]
        xn = [reg(take(2 * K), 2 * K, BF16) for _ in range(3)]
        wst_off = [take(4 * K) for _ in range(3)]
        wst = [reg(wst_off[i], 4 * K, F32).rearrange("p (a b) -> p a b", a=KC) for i in range(3)]
        wb_extra = [reg(wst_off[2] + i * 2 * K, 2 * K, BF16).rearrange("p (a b) -> p a b", a=KC) for i in range(2)]
        wbt = [reg(take(2 * K), 2 * K, BF16).rearrange("p (a b) -> p a b", a=KC) for _ in range(3)]
        att_t = reg(take(1024), 1024, F32)
        junk = reg(take(2 * K), 2 * K, BF16)
        assert o <= ARENA_BYTES, o
        ontmp = junk.bitcast(F32).rearrange("p (q e) -> p q e", q=4)
        o = base_ph
        oT = reg(take(32 * K), 32 * K, F32).rearrange("p (c t) -> p c t", c=4)
        assert o <= kq_off
        o = HT0 + 32 * K
        gattnT = reg(take(16 * K), 16 * K, BF16).rearrange("p (c t) -> p c t", c=4)
        gconvT = reg(take(16 * K), 16 * K, BF16).rearrange("p (c t) -> p c t", c=4)
        assert o <= HT0 + 64 * K
        o = kq_off
        prod = [reg(take(4 * 514 * 4), 4 * 514 * 4, F32).rearrange("p (t n) -> p t n", t=4)]
        tmp = [reg(take(2 * K), 2 * K, F32) for _ in range(8)]
        gpost = reg(take(4 * K), 4 * K, F32)
        assert o <= oraw_off, (o, oraw_off)
        yT = reg(oraw_off, 32 * K, BF16).rearrange("p (j t) -> p j t", j=KC)
        o = ph01_tail
        wao = reg(take(8 * K), 8 * K, BF16).rearrange("p (c n) -> p c n", c=4)
        wco = reg(take(8 * K), 8 * K, BF16).rearrange("p (c n) -> p c n", c=4)
        assert o <= ph01_tail + 16 * K
        o = base_ph
        wo = reg(take(16 * K), 16 * K, BF16).rearrange("p (j n) -> p j n", j=KC)
        assert o <= base_ph + 32 * K
        NOS = 4
        ostage = [reg(HT0 + 32 * K + i * 4 * K, 4 * K, F32) for i in range(NOS)]
        xres = [reg(HT0 + 48 * K + i * 4 * K, 4 * K, F32) for i in range(NOS)]

        S_EPSD = small[:, 0:1]
        S_EPSO = small[:, 1:2]
        S_XB = small[:, 2:3]
        S_GCOL = small[:, 3:4]
        S_S1 = small[:, 4:5]
        S_S2 = small[:, 5:6]
        S_E1 = small[:, 6:7]
        S_E2 = small[:, 7:8]
        S_NLAM = small[:, 8:9]
        S_GS = small[:, 9:10]
        S_R = small[:, 16:32]

        ss0 = stats[:, 0:40]
        sd0 = stats[:, 40:80]
        rs0 = stats[:, 80:120]
        sso = stats[:, 120:184]
        sdo = stats[:, 184:248]
        p = Prog(nc, st)

        def bview_bf16(bank):
            return bank[:].bitcast(BF16)

        def ld(eng, out_ap, in_ap, res, stream):
            return p.op(eng, lambda e: e.dma_start(out=out_ap, in_=in_ap), writes=[res], dma=stream)

        for i0 in range(2):
            p.op("sp", (lambda e, i0=i0: e.dma_start(out=xring[i0], in_=x_perm[i0 * 128:(i0 + 1) * 128, :])),
                 writes=[("x", i0)], dma=f"ldx{i0}")
        ld("sp", gb, gb_d.rearrange("p (a b) -> p a b", a=KC), "gb", "ldgb")
        ld("sp", identb, identb_d, "identb", "ldib")
        ld("sp", tri, tri_d, "tri", "G:ldc")
        ld("sp", identf, identf_d, "identf", "G:ldc")
        ld("sp", lamt, lamv.partition_broadcast(128), "lamt", "G:ldc")
        ld("sp", cw[:, 0:12], cw_d, "cw", "G:ldc")
        ld("sp", bm[:, 0:16], bm_d, "bm", "G:ldc")
        ld("sp", S_XB, xbias_d, "xb", "G:ldc")
        ld("sp", S_GS, gsub, "gs", "G:ldc")
        p.op("pool", lambda e: e.memset(S_EPSD, EPS), writes=["epsd"])
        p.op("pool", lambda e: e.memset(S_EPSO, EPS), writes=["epso"])
        p.op("dve", lambda e: e.tensor_scalar(out=S_GCOL, in0=S_GS, scalar1=float(1.0 - LAM_INIT), scalar2=None, op0=ALU.mult),
             reads=["gs"], writes=["gcol"])
        p.op("dve", lambda e: e.scalar_tensor_tensor(out=junk.bitcast(F32)[:, 0:64], in0=lamt[:, 0:64], scalar=1.0, in1=lamt[:, 64:128],
                                                     op0=ALU.mult, op1=ALU.mult, accum_out=S_S1),
             reads=["lamt"], writes=["s1"])
        p.op("dve", lambda e: e.scalar_tensor_tensor(out=junk.bitcast(F32)[:, 64:128], in0=lamt[:, 128:192], scalar=1.0, in1=lamt[:, 192:256],
                                                     op0=ALU.mult, op1=ALU.mult, accum_out=S_S2),
             reads=["lamt"], writes=["s2"])
        p.op("act", lambda e: e.activation(out=S_E1, in_=S_S1, func=AF.Exp), reads=["s1"], writes=["e1"])
        p.op("act", lambda e: e.activation(out=S_E2, in_=S_S2, func=AF.Exp), reads=["s2"], writes=["e2"])
        p.op("dve", lambda e: e.tensor_tensor(out=S_NLAM, in0=S_E2, in1=S_E1, op=ALU.subtract), reads=["e1", "e2"], writes=["nl0"])
        p.op("dve", lambda e: e.tensor_scalar(out=S_NLAM, in0=S_NLAM, scalar1=float(-LAM_INIT), scalar2=None, op0=ALU.add),
             reads=["nl0"], writes=["nlam"])

        wctr = [0]

        def load_w_chunk(c, dst_ap, dst_res, eng="pool", xslot=None, defer=None):
            if xslot is None:
                s = wctr[0] % 3
                wctr[0] += 1
                stg, sres, strm = wst[s], ("wst", s), f"ldw{s}"
            else:
                stg, sres, strm = xring[xslot].rearrange("p (a b) -> p a b", a=KC), ("x", xslot), f"ldx{xslot}"
            p.op("sp", (lambda e, c=c, stg=stg: e.dma_start(out=stg, in_=w_in_r[c].rearrange("p (a b) -> p a b", a=KC))),
                 writes=[sres], dma=strm)
            if defer is not None:
                defer.append(lambda: cast_w_chunk(stg, sres, dst_ap, dst_res, eng))
                return
            cast_w_chunk(stg, sres, dst_ap, dst_res, eng)

        def cast_w_chunk(stg, sres, dst_ap, dst_res, eng):
            if eng == "act":
                for kc in range(KC):
                    p.op("act", (lambda e, stg=stg, dst_ap=dst_ap, kc=kc: e.activation(out=dst_ap[:, kc, :], in_=stg[:, kc, :], func=AF.Copy,
                                                                                      scale=gb[:, kc, 0:1])),
                         reads=[sres, "gb"], writes=[dst_res])
            else:
                p.op(eng, (lambda e, stg=stg, dst_ap=dst_ap: e.tensor_tensor(out=dst_ap, in0=stg, in1=gb, op=ALU.mult)),
                     reads=[sres, "gb"], writes=[dst_res])

        wv_all = reg(ph01_tail, 8 * K, BF16).rearrange("p (a b) -> p a b", a=KC)
        for c in range(3):
            load_w_chunk(8 + c, wv_all[:, :, c * 128:(c + 1) * 128], ("wv", c), eng="dve")
        X_PREISSUE = True
        p.op("pool", lambda e: e.memset(V[:, :, :, 128:129], 1.0), writes=["Vones"])

        def late_setup():
            for m in range(2):
                p.op("sp", (lambda e, m=m: e.dma_start(out=KTa[m][64:128, :], in_=zeros_d)), writes=[("KTpad", m)], dma=f"ldzk{m}")
                p.op("sp", (lambda e, m=m: e.dma_start(out=QTa[m][64:128, :], in_=zeros_d[:, 0:NOWN])), writes=[("QTpad", m)], dma=f"ldzq{m}")
                p.op("sp", (lambda e, m=m: e.dma_start(out=KTa[m][64:72, :], in_=kaug_d)), reads=[("KTpad", m)],
                     writes=[("KTaug", m)], dma=f"ldk{m}")
            for m in range(2):
                p.op("sp", (lambda e, m=m: e.dma_start(out=QTa[m][64:72, :], in_=qaug_d[0:8, :])),
                     reads=[("QTpad", m)], writes=[("QTaug", m)], dma=f"ldq{m}")

        def kq_tile(kind, tt, wsrc, wres_, bk):
            dstT = KTa if kind == "k" else QTa
            rname = "KT" if kind == "k" else "QT"
            for kc in range(KC):
                p.op("pe", (lambda e, kc=kc, bk=bk, tt=tt, wsrc=wsrc: e.matmul(
                    banks[bk][:], lhsT=wsrc[:, kc, :], rhs=hT_tile(kc, tt),
                    start=(kc == 0), stop=(kc == KC - 1))),
                     reads=[wres_, ("hT", tt)], writes=[("bank", bk)])
            p.op("dve", (lambda e, bk=bk, tt=tt, dstT=dstT: e.tensor_copy(out=dstT[0][0:64, tt * 512:(tt + 1) * 512],
                                                                       in_=banks[bk][0:64, :])),
                 reads=[("bank", bk)], writes=[(rname, 0)])
            p.op("act", (lambda e, bk=bk, tt=tt, dstT=dstT: e.copy(out=dstT[1][0:64, tt * 512:(tt + 1) * 512],
                                                                 in_=banks[bk][64:128, :])),
                 reads=[("bank", bk)], writes=[(rname, 1)])

        def vproj(i):
            bk = 4 + i % 2
            for kc in range(KC):
                p.op("pe", (lambda e, i=i, kc=kc, bk=bk: e.matmul(
                    banks[bk][:], lhsT=hT_blk(kc, i), rhs=wv_all[:, kc, :],
                    start=(kc == 0), stop=(kc == KC - 1))),
                     reads=[("hT", i // 4)] + [("wv", c) for c in range(4)], writes=[("bank", bk)])
            dstv = V[:, i, :, 0:128]
            srcv = (lambda bk=bk: banks[bk][:].rearrange("p (h e) -> p h e", h=4))
            if i % 2 == 0:
                p.op("act", (lambda e, dstv=dstv, srcv=srcv: e.copy(out=dstv, in_=srcv())), reads=[("bank", bk)], writes=[("V", i), "Vall"])
            else:
                p.op("dve", (lambda e, dstv=dstv, srcv=srcv: e.tensor_copy(out=dstv, in_=srcv())), reads=[("bank", bk)], writes=[("V", i), "Vall"])

        def follow(j):
            if 0 <= j < 32:
                vproj(j)
                if j % 4 == 3:
                    tt = j // 4
                    kq_tile("k", tt, wph0[5], ("wk0",), 6)
                    if tt < 4:
                        kq_tile("q", tt, wph0[4], ("wq0",), 7)

        NBLK = 33
        LAG = 6
        RAMP = {2: 0, 3: 1, 5: 2, 7: 3}
        RAMP_END = 10
        PRE = 2
        NXN = 3

        def stage_a(i):
            rows = 128 if i < 32 else 8
            xb_ = i % NXR
            nb = i % NXN
            p.op("act", (lambda e, xb_=xb_, rows=rows, i=i: e.activation(out=junk[0:rows, :], in_=xring[xb_][0:rows, :],
                                                                        func=AF.Square, accum_out=ss0[0:rows, i:i + 1])),
                 reads=[("x", xb_)], writes=[("ss0", i)])
            p.op("act", (lambda e, rows=rows, i=i: e.activation(out=sd0[0:rows, i:i + 1], in_=ss0[0:rows, i:i + 1], func=AF.Sqrt,
                                                                bias=S_EPSD[0:rows, :], scale=1.0 / D)),
                 reads=[("ss0", i), "epsd"], writes=[("sd0", i)])
            p.op("dve", (lambda e, rows=rows, i=i: e.reciprocal(out=rs0[0:rows, i:i + 1], in_=sd0[0:rows, i:i + 1])),
                 reads=[("sd0", i)], writes=[("rs0", i)])
            p.op("dve", (lambda e, rows=rows, i=i, xb_=xb_, nb=nb: e.tensor_scalar(out=xn[nb][0:rows, :], in0=xring[xb_][0:rows, :],
                                                                                   scalar1=rs0[0:rows, i:i + 1], scalar2=None, op0=ALU.mult)),
                 reads=[("x", xb_), ("rs0", i)], writes=[("xn", nb)])

        def x_load(i):
            if i >= NBLK:
                return
            rows = 128 if i < 32 else 8
            xb_ = i % NXR
            src = x_perm[i * 128:(i + 1) * 128, :] if i < 32 else x_halo
            p.op("sp", (lambda e, xb_=xb_, rows=rows, src=src: e.dma_start(out=xring[xb_][0:rows, :], in_=src)),
                 writes=[("x", xb_)], dma=f"ldx{xb_}")

        def stage_b(i):
            rows = 128 if i < 32 else 8
            nb = i % NXN
            bk = i % 4
            for kc in range(KC):
                p.op("pe", (lambda e, kc=kc, rows=rows, nb=nb, bk=bk: e.transpose(
                    bview_bf16(banks[bk])[:, kc * 128:kc * 128 + rows], xn[nb][0:rows, kc * 128:(kc + 1) * 128],
                    identb[0:rows, 0:rows])),
                     reads=[("xn", nb), "identb"], writes=[("bank", bk)])
            if i < 32:
                dst = hT2[i // 16][:, :, (i % 16) * 128:(i % 16 + 1) * 128]
                srcp = (lambda bk=bk: bview_bf16(banks[bk]).rearrange("p (a b) -> p a b", a=KC))
                wres = ("hT", i // 4)
            else:
                dst = hTh
                srcp = (lambda bk=bk: bview_bf16(banks[bk]).rearrange("p (a b) -> p a b", a=KC)[:, :, 0:8])
                wres = "hTh"
            if i % 2 == 1:
                p.op("act", (lambda e, dst=dst, srcp=srcp: e.copy(out=dst, in_=srcp())), reads=[("bank", bk)], writes=[wres])
            else:
                p.op("dve", (lambda e, dst=dst, srcp=srcp: e.tensor_copy(out=dst, in_=srcp())), reads=[("bank", bk)], writes=[wres])
            if i in RAMP:
                follow(RAMP[i])
            elif i >= RAMP_END:
                follow(i - LAG)

        deferred_casts = []
        for i in range(2, 5):
            x_load(i)
        load_w_chunk(8 + 3, wv_all[:, :, 3 * 128:4 * 128], ("wv", 3), eng="dve", xslot=5)
        load_w_chunk(4, wph0[5], ("wk0",), eng="dve", xslot=6, defer=deferred_casts)
        load_w_chunk(0, wph0[4], ("wq0",), eng="dve", xslot=7, defer=deferred_casts)
        x_load(5)
        for i in range(PRE):
            stage_a(i)
        for i in range(NBLK):
            if i + PRE < NBLK:
                stage_a(i + PRE)
            x_load(i + NXR)
            stage_b(i)
            if i == 2:
                for f_ in deferred_casts:
                    f_()
                x_load(6)
                x_load(7)
            if i == 5:
                late_setup()
        for j in range(NBLK - LAG, 32):
            follow(j)

        dbg(p, "hT", lambda: hT2[0][:, :, 0:512], [128, 8, 512], BF16, [("hT", 0)])
        dbg(p, "hTh", lambda: hTh, [128, 8, 8], BF16, ["hTh"])
        dbg(p, "V", lambda: V[:, 0:4, :, :], [128, 4, 4, 129], BF16, ["Vall", "Vones"])
        def sbank(m, buf):
            return banks[2 * buf + m]

        def sbank_id(m, buf):
            return 2 * buf + m

        def oacc(m, qb):
            a = m * 4 + qb
            return banks[4 + a // 3][:, (a % 3) * 129:(a % 3) * 129 + 129]

        def oacc_id(m, qb):
            return 4 + (m * 4 + qb) // 3

        ptctr = [0]
        def head_slots(h):
            if h % 2 == 1:
                return wbt[2], xn[1].rearrange("p (a b) -> p a b", a=KC), ("wq", 1), ("wk", 1), [], [("xn", 1)]
            return wbt[0], wbt[1], ("wq", 0), ("wk", 0), [], []

        def load_head_weights(h):
            wq, wk, wq_res, wk_res, extra_q, extra_k = head_slots(h)
            for c_, dst_, res_, ex_ in ((h, wq, wq_res, extra_q), (4 + h, wk, wk_res, extra_k)):
                s = wctr[0] % 3
                wctr[0] += 1
                p.op("sp", (lambda e, c_=c_, s=s: e.dma_start(out=wst[s], in_=w_in_r[c_].rearrange("p (a b) -> p a b", a=KC))),
                     writes=[("wst", s)], dma=f"ldw{s}")
                p.op("pool", (lambda e, s=s, dst_=dst_: e.tensor_tensor(out=dst_, in0=wst[s], in1=gb, op=ALU.mult)),
                     reads=[("wst", s), "gb"], writes=[res_] + ex_)

        W2_ORDER = [12, 13, 14, 15]
        for c_ in range(4):
            W2_ORDER += [20 + c_, 24 + c_, 16 + c_, 28 + c_]
        for j_ in range(KC):
            W2_ORDER += [32 + j_, 40 + j_]
        W2_SLOTS = [wbt[0], wbt[1], wbt[2], wb_extra[0], wb_extra[1]]
        w2_state = {"dma": 0, "cast": 0, "next": 0}
        W2_LOOK = 2

        W2_FIRST = {0: [("wq", 0)], 1: [("wk", 0)], 2: [("wq", 1)], 3: [("wst", 2)], 4: [("wst", 2)]}

        def w2_dma_until(n_last):
            while w2_state["dma"] <= min(n_last, len(W2_ORDER) - 1):
                n_ = w2_state["dma"]
                w2_state["dma"] += 1
                sg = n_ % 2
                p.op("sp", (lambda e, c=W2_ORDER[n_], sg=sg: e.dma_start(out=wst[sg], in_=w_in_r[c].rearrange("p (a b) -> p a b", a=KC))),
                     writes=[("wst", sg)], dma=f"ldw{sg}")

        def w2_cast_until(n_last):
            while w2_state["cast"] <= min(n_last, len(W2_ORDER) - 1):
                n_ = w2_state["cast"]
                w2_dma_until(n_ + 1)
                w2_state["cast"] += 1
                sg = n_ % 2
                sl = n_ % 5
                extra_w = W2_FIRST[sl] if n_ < 5 else []
                if n_ < 4:
                    p.op("dve", (lambda e, sg=sg, sl=sl: e.tensor_tensor(out=W2_SLOTS[sl], in0=wst[sg], in1=gb, op=ALU.mult)),
                         reads=[("wst", sg), "gb"], writes=[("wb2", sl)] + extra_w)
                    continue
                for kc in range(KC):
                    p.op("act", (lambda e, sg=sg, sl=sl, kc=kc: e.activation(out=W2_SLOTS[sl][:, kc, :], in_=wst[sg][:, kc, :],
                                                                              func=AF.Copy, scale=gb[:, kc, 0:1])),
                         reads=[("wst", sg), "gb"], writes=[("wb2", sl)] + extra_w)

        def w2_get():
            k = w2_state["next"]
            w2_state["next"] += 1
            w2_cast_until(k + W2_LOOK)
            return W2_SLOTS[k % 5], ("wb2", k % 5), W2_ORDER[k]

        def hn_items(hh):
            items = []

            def stats_():
                p.op("act", (lambda e: e.activation(out=sdo[:, hh * 16:(hh + 1) * 16], in_=sso[:, hh * 16:(hh + 1) * 16], func=AF.Sqrt,
                                                    bias=S_EPSO, scale=1.0 / 128)),
                     reads=[("sso", hh * 16 + b_) for b_ in range(16)] + ["epso"], writes=[("sdo", hh)])
                p.op("dve", (lambda e: e.reciprocal(out=sdo[:, hh * 16:(hh + 1) * 16], in_=sdo[:, hh * 16:(hh + 1) * 16])),
                     reads=[("sdo", hh)], writes=[("rso", hh)])
            items.append(stats_)
            for g in range(4):
                def scale_(g=g):
                    for q in range(4):
                        blk = 4 * g + q
                        col = hh * 16 + blk
                        p.op("dve", (lambda e, q=q, blk=blk, col=col: e.tensor_scalar(
                            out=ontmp[:, q, :], in0=oraw[:, hh, blk, :], scalar1=sdo[:, col:col + 1], scalar2=None, op0=ALU.mult)),
                             reads=[("rso", hh), ("oraw", hh, blk)], writes=[("ontmp", q)])

                def tr_(g=g):
                    for q in range(4):
                        p.op("pe", (lambda e, q=q: e.transpose(banks[7][:, q * 128:(q + 1) * 128], ontmp[:, q, :], identf)),
                             reads=[("ontmp", q), "identf"], writes=[("bank", 7)])
                    p.op("dve", (lambda e, g=g: e.tensor_scalar(out=oTn[hh][:, g * 512:(g + 1) * 512], in0=banks[7][:],
                                                               scalar1=S_GCOL, scalar2=None, op0=ALU.mult)),
                         reads=[("bank", 7), "gcol"], writes=[("oT", hh, g)] + [("oraw", hh, 4 * g + q_) for q_ in range(4)])
                items.append(scale_)
                items.append(tr_)
            return items

        load_head_weights(1)
        epi_ctr = [0]
        for h in range(4):
            hn_work = hn_items(h - 1) if h >= 1 else []
            if hn_work:
                hn_work.pop(0)()
            if h >= 1:
                wq, wk, wq_res, wk_res, _, _ = head_slots(h)
                for m in range(2):
                    p.op("sp", (lambda e, m=m, h=h: e.dma_start(out=QTa[m][64:72, :], in_=qaug_d[8 * h:8 * h + 8, :])),
                         reads=[("QTpad", m)], writes=[("QTaug", m)], dma=f"ldq{m}")
                for tt in range(8):
                    kq_tile("k", tt, wk, wk_res, tt % 4)
                for tt in range(4):
                    kq_tile("q", tt, wq, wq_res, tt % 4)
                if h + 1 < 4:
                    load_head_weights(h + 1)
            ucount = [0]
            for s_ in range(4):
                units = []
                for dl in range(4):
                    units.append((16 + 4 * s_ + dl, 0, False, True))
                for t in range(s_):
                    for dl in range(4):
                        units.append((4 * t + dl, 0, False, False))
                    for dl in range(4):
                        units.append((16 + 4 * t + dl, 0, False, False))
                for dl in range(4):
                    units.append((4 * s_ + dl, 128 * dl, True, False))
                nu = len(units)
                first_acc = [True]
                acc_started = {}

                def emit_qk(u):
                    j, c0, diag, xm = units[u]
                    buf = u % 2
                    for m in range(2):
                        bid = sbank_id(m, buf)
                        p.op("pe", (lambda e, m=m, j=j, c0=c0, buf=buf, diag=diag, s_=s_: e.matmul(
                            sbank(m, buf)[:, c0:512], lhsT=KTa[m][:, j * 128:(j + 1) * 128],
                            rhs=QTa[m][:, s_ * 512 + c0:(s_ + 1) * 512], start=True, stop=(not diag))),
                             reads=[("KT", m), ("KTaug", m), ("KTpad", m), ("QT", m), ("QTaug", m), ("QTpad", m)],
                             writes=[("bank", bid)])
                        if diag:
                            p.op("pe", (lambda e, m=m, c0=c0, buf=buf: e.matmul(
                                sbank(m, buf)[:, c0:c0 + 128], lhsT=identb, rhs=tri, start=False, stop=True)),
                                 reads=["identb", "tri"], writes=[("bank", bid)])

                def emit_exp(u):
                    j, c0, diag, xm = units[u]
                    buf = u % 2
                    res = []
                    for m in range(2):
                        bid = sbank_id(m, buf)
                        pi = ptctr[0] % NPT
                        res.append(pi)
                        p.op("act", (lambda e, m=m, buf=buf, c0=c0, pi=pi: e.activation(
                            out=PT[m][pi][:, c0:512], in_=sbank(m, buf)[:, c0:512], func=AF.Exp, scale=0.125)),
                             reads=[("bank", bid)], writes=[("PT", m, pi)])
                    ptctr[0] += 1
                    return res[0]

                def emit_av(u, pi):
                    j, c0, diag, xm = units[u]
                    for m in range(2):
                        for qb in range(c0 // 128, 4):
                            bid = oacc_id(m, qb)
                            st_flag = bid not in acc_started
                            acc_started[bid] = True
                            p.op("pe", (lambda e, m=m, qb=qb, j=j, pi=pi, st_flag=st_flag, h=h, lastu=(u == nu - 1): e.matmul(
                                oacc(m, qb), lhsT=PT[m][pi][:, qb * 128:(qb + 1) * 128], rhs=V[:, j, h, :],
                                start=st_flag, stop=lastu, skip_group_check=True)),
                                 reads=[("PT", m, pi), "Vall", "Vones"], writes=[("bank", bid)])

                pis = {}
                emit_qk(0)
                for u in range(nu):
                    pis[u] = emit_exp(u)
                    if u + 1 < nu:
                        emit_qk(u + 1)
                    emit_av(u, pis[u])
                    ucount[0] += 1
                    if hn_work and ucount[0] >= 8 and ucount[0] % 6 == 2:
                        hn_work.pop(0)()
                    if h == 3:
                        if ucount[0] == 5:
                            w2_dma_until(1)
                        elif ucount[0] in (15, 25, 35, 45):
                            w2_cast_until((ucount[0] - 15) // 10)

                ob = epi_ctr[0] % 2
                first = epi_ctr[0] < 2
                epi_ctr[0] += 1
                for bnk, a0, a1 in ((4, 0, 3), (5, 3, 6), (6, 6, 8)):
                    n_ = a1 - a0
                    p.op("dve", (lambda e, bnk=bnk, a0=a0, a1=a1, n_=n_, ob=ob: e.tensor_copy(
                        out=ocp[ob][:, a0:a1, :], in_=banks[bnk][:, 0:n_ * 129].rearrange("p (a e) -> p a e", a=n_))),
                         reads=[("bank", bnk)],
                         writes=[("ocp", ob, bnk)] + ([("wv", c) for c in range(4)] + [("wq0",), ("wk0",)] if first else []))
                ocp_res = [("ocp", ob, 4), ("ocp", ob, 5), ("ocp", ob, 6)]
                rl = S_R[:, 0:8]
                p.op("dve", (lambda e, ob=ob, rl=rl: e.reciprocal(out=rl.rearrange("p (a o) -> p a o", o=1), in_=ocp[ob][:, :, 128:129])),
                     reads=ocp_res, writes=["rl"])
                p.op("dve", (lambda e: e.tensor_scalar(out=S_R[:, 8:12], in0=S_R[:, 4:8], scalar1=S_NLAM, scalar2=None, op0=ALU.mult)),
                     reads=["rl", "nlam"], writes=["r1n"])
                for qb in range(4):
                    blk = s_ * 4 + qb
                    col = h * 16 + blk
                    p.op("dve", (lambda e, qb=qb, ob=ob: e.tensor_scalar(out=att_t[:, 0:128], in0=ocp[ob][:, qb, 0:128], scalar1=S_R[:, qb:qb + 1],
                                                                        scalar2=None, op0=ALU.mult)),
                         reads=ocp_res + ["rl"], writes=["att_t"])
                    xw = [("x", k_) for k_ in range(NXR)] if (h == 0 and s_ == 0) else []
                    p.op("dve", (lambda e, qb=qb, blk=blk, h=h, ob=ob: e.scalar_tensor_tensor(
                        out=oraw[:, h, blk, :], in0=ocp[ob][:, 4 + qb, 0:128], scalar=S_R[:, 8 + qb:9 + qb], in1=att_t[:, 0:128],
                        op0=ALU.mult, op1=ALU.add)),
                         reads=ocp_res + ["r1n", "att_t"], writes=[("oraw", h, blk)] + xw)
                    p.op("dve", (lambda e, blk=blk, col=col, h=h: e.scalar_tensor_tensor(
                        out=att_t[:, 128:256], in0=oraw[:, h, blk, :], scalar=1.0, in1=oraw[:, h, blk, :],
                        op0=ALU.mult, op1=ALU.mult, accum_out=sso[:, col:col + 1])),
                         reads=[("oraw", h, blk)], writes=[("sso", col), "att_t2"])

            while hn_work:
                hn_work.pop(0)()

        for m in range(2):
            dbg(p, f"KTa{m}", (lambda m=m: KTa[m]), [128, NT], BF16, [("KT", m), ("KTaug", m), ("KTpad", m)])
            dbg(p, f"QTa{m}", (lambda m=m: QTa[m]), [128, NOWN], BF16, [("QT", m), ("QTaug", m), ("QTpad", m)])
        dbg(p, "oraw", lambda: oraw, [128, 4, 16, 128], F32, [("oraw", h, b) for h in range(4) for b in range(16)])
        dbg(p, "small", lambda: small, [128, 256], F32, ["nlam", "gcol"])
        for f_ in hn_items(3):
            f_()
        bctr = [0]

        def next_bank():
            b = bctr[0] % 8
            bctr[0] += 1
            return b

        tctr2 = [0]

        def next_tmp():
            t = tctr2[0] % 8
            tctr2[0] += 1
            return t

        def proj_tile(wsrc, wres_, tt, bk):
            for kc in range(KC):
                p.op("pe", (lambda e, kc=kc, bk=bk, tt=tt, wsrc=wsrc: e.matmul(
                    banks[bk][:], lhsT=wsrc[:, kc, :], rhs=hT_tile(kc, tt),
                    start=(kc == 0), stop=(kc == KC - 1))),
                     reads=[wres_, ("hT", tt)], writes=[("bank", bk)])

        on_list = [(h_, g_) for h_ in range(4) for g_ in range(4)]
        NPRE = 7
        za_w = {}
        pre_bank = {}
        for idx in range(NPRE):
            c, tt = on_list[idx]
            if tt == 0:
                za_w[c] = w2_get()
                assert za_w[c][2] == 12 + c
            pre_bank[idx] = next_bank()
            proj_tile(za_w[c][0], za_w[c][1], tt, pre_bank[idx])
        p.barrier()
        p.op("pool", lambda e: e.dma_start(out=wao, in_=wao_r.rearrange("p (c n) -> p c n", c=4)), writes=["wao"], dma="ldpa")
        p.op("pool", lambda e: e.dma_start(out=wco, in_=wco_r.rearrange("p (c n) -> p c n", c=4)), writes=["wco"], dma="ldpc")
        ld("sp", gpost, gpost_d.partition_broadcast(128), "gpost", "ldg")
        for idx, (c, tt) in enumerate(on_list):
            if idx >= NPRE and tt == 0:
                za_w[c] = w2_get()
                assert za_w[c][2] == 12 + c
            if idx in pre_bank:
                bk = pre_bank[idx]
            else:
                bk = next_bank()
                proj_tile(za_w[c][0], za_w[c][1], tt, bk)
            ti = next_tmp()
            p.op("act", (lambda e, bk=bk, ti=ti: e.activation(out=tmp[ti], in_=banks[bk][:], func=AF.Silu)),
                 reads=[("bank", bk)], writes=[("tmp", ti)])
            p.op("dve", (lambda e, c=c, tt=tt, ti=ti: e.tensor_tensor(out=gattnT[:, c, tt * 512:(tt + 1) * 512], in0=tmp[ti],
                                                                     in1=oTn[c][:, tt * 512:(tt + 1) * 512], op=ALU.mult)),
                 reads=[("tmp", ti), ("oT", c, tt)], writes=[("gattnT", c, tt)])

        pr = prod[0]
        for c in range(4):
            wcb, wcc, wcx, wzc = None, None, None, None
            W_ws_cc, R_ws_cc, cid_ = w2_get()
            assert cid_ == 20 + c
            W_ws_cx, R_ws_cx, cid_ = w2_get()
            assert cid_ == 24 + c
            bkh = next_bank()
            for kc in range(KC):
                p.op("pe", (lambda e, kc=kc, bkh=bkh, W_ws_cc=W_ws_cc: e.matmul(
                    banks[bkh][:, 0:8], lhsT=W_ws_cc[:, kc, :], rhs=hTh[:, kc, :], start=(kc == 0), stop=(kc == KC - 1),
                    skip_group_check=True)),
                     reads=[R_ws_cc, "hTh"], writes=[("bank", bkh)])
            for kc in range(KC):
                p.op("pe", (lambda e, kc=kc, bkh=bkh, W_ws_cx=W_ws_cx: e.matmul(
                    banks[bkh][:, 8:16], lhsT=W_ws_cx[:, kc, :], rhs=hTh[:, kc, :], start=False, stop=(kc == KC - 1),
                    skip_group_check=True)),
                     reads=[R_ws_cx, "hTh"], writes=[("bank", bkh)])
            tih = next_tmp()
            p.op("act", (lambda e, bkh=bkh, tih=tih: e.copy(out=tmp[tih][:, 0:8], in_=banks[bkh][:, 0:8])),
                 reads=[("bank", bkh)], writes=[("tmp", tih)])
            p.op("dve", (lambda e, bkh=bkh, tih=tih: e.tensor_tensor(
                out=pr[:, :, 0:2], in0=banks[bkh][:, 8:16].rearrange("p (t n) -> p t n", t=4),
                in1=tmp[tih][:, 0:8].rearrange("p (t n) -> p t n", t=4), op=ALU.mult)),
                 reads=[("bank", bkh), ("tmp", tih)], writes=["prod_h"])
            for tt in range(4):
                bk1 = next_bank()
                proj_tile(W_ws_cc, R_ws_cc, tt, bk1)
                bk2 = next_bank()
                proj_tile(W_ws_cx, R_ws_cx, tt, bk2)
                ti = next_tmp()
                p.op("act", (lambda e, bk1=bk1, ti=ti: e.copy(out=tmp[ti], in_=banks[bk1][:])),
                     reads=[("bank", bk1)], writes=[("tmp", ti)])
                p.op("dve", (lambda e, bk2=bk2, ti=ti, tt=tt: e.tensor_tensor(out=pr[:, tt, 2:514], in0=banks[bk2][:], in1=tmp[ti],
                                                                             op=ALU.mult)),
                     reads=[("bank", bk2), ("tmp", ti)], writes=[("prod", tt)])
            W_ws_cb, R_ws_cb, cid_ = w2_get()
            assert cid_ == 16 + c
            utiles = []
            for tt in range(4):
                tu = next_tmp()
                utiles.append(tu)
                p.op("dve", (lambda e, tt=tt, tu=tu, c=c: e.tensor_scalar(out=tmp[tu], in0=pr[:, tt, 0:512], scalar1=cw[:, 3 * c:3 * c + 1],
                                                                         scalar2=None, op0=ALU.mult)),
                     reads=[("prod", tt), "prod_h", "cw"], writes=[("tmp", tu)])
                p.op("dve", (lambda e, tt=tt, tu=tu, c=c: e.scalar_tensor_tensor(out=tmp[tu], in0=pr[:, tt, 1:513],
                                                                                scalar=cw[:, 3 * c + 1:3 * c + 2], in1=tmp[tu],
                                                                                op0=ALU.mult, op1=ALU.add)),
                     reads=[("prod", tt), "prod_h", "cw", ("tmp", tu)], writes=[("tmp", tu)])
                p.op("dve", (lambda e, tt=tt, tu=tu, c=c: e.scalar_tensor_tensor(out=tmp[tu], in0=pr[:, tt, 2:514],
                                                                                scalar=cw[:, 3 * c + 2:3 * c + 3], in1=tmp[tu],
                                                                                op0=ALU.mult, op1=ALU.add)),
                     reads=[("prod", tt), "cw", ("tmp", tu)], writes=[("tmp", tu)])
                bk = next_bank()
                proj_tile(W_ws_cb, R_ws_cb, tt, bk)
                p.op("dve", (lambda e, bk=bk, tu=tu: e.tensor_tensor(out=tmp[tu], in0=banks[bk][:], in1=tmp[tu], op=ALU.mult)),
                     reads=[("bank", bk), ("tmp", tu)], writes=[("tmp", tu)])
            W_ws_zc, R_ws_zc, cid_ = w2_get()
            assert cid_ == 28 + c
            for tt in range(4):
                tu = utiles[tt]
                bk = next_bank()
                proj_tile(W_ws_zc, R_ws_zc, tt, bk)
                ti = next_tmp()
                while ti in utiles:
                    ti = next_tmp()
                p.op("act", (lambda e, bk=bk, ti=ti: e.activation(out=tmp[ti], in_=banks[bk][:], func=AF.Silu)),
                     reads=[("bank", bk)], writes=[("tmp", ti)])
                p.op("dve", (lambda e, c=c, tt=tt, ti=ti, tu=tu: e.tensor_tensor(out=gconvT[:, c, tt * 512:(tt + 1) * 512], in0=tmp[tu],
                                                                                in1=tmp[ti], op=ALU.mult)),
                     reads=[("tmp", ti), ("tmp", tu)], writes=[("gconvT", c, tt)])

        oT_all = [("oT", h, g) for h in range(4) for g in range(4)]
        p.op("pool", lambda e: e.dma_start(out=wo, in_=wo_r.rearrange("p (j n) -> p j n", j=KC)), writes=["wo"] + oT_all, dma="ldpo")
        for j in range(KC):
            W_ws_a, R_ws_a, cid_ = w2_get()
            assert cid_ == 32 + j
            W_ws_c, R_ws_c, cid_ = w2_get()
            assert cid_ == 40 + j
            for tt in range(4):
                bka = next_bank()
                proj_tile(W_ws_a, R_ws_a, tt, bka)
                bkc = next_bank()
                proj_tile(W_ws_c, R_ws_c, tt, bkc)
                bya = next_bank()
                for c in range(4):
                    p.op("pe", (lambda e, c=c, j=j, tt=tt, bya=bya: e.matmul(
                        banks[bya][:], lhsT=wao[:, c, j * 128:(j + 1) * 128], rhs=gattnT[:, c, tt * 512:(tt + 1) * 512],
                        start=(c == 0), stop=(c == 3))),
                         reads=["wao", ("gattnT", c, tt)], writes=[("bank", bya)])
                byc = next_bank()
                for c in range(4):
                    p.op("pe", (lambda e, c=c, j=j, tt=tt, byc=byc: e.matmul(
                        banks[byc][:], lhsT=wco[:, c, j * 128:(j + 1) * 128], rhs=gconvT[:, c, tt * 512:(tt + 1) * 512],
                        start=(c == 0), stop=(c == 3))),
                         reads=["wco", ("gconvT", c, tt)], writes=[("bank", byc)])
                ta = next_tmp()
                tc = next_tmp()
                p.op("act", (lambda e, bka=bka, ta=ta, j=j: e.activation(out=tmp[ta], in_=banks[bka][:], func=AF.Sigmoid,
                                                                        bias=bm[:, j:j + 1])),
                     reads=[("bank", bka), "bm"], writes=[("tmp", ta)])
                p.op("act", (lambda e, bkc=bkc, tc=tc, j=j: e.activation(out=tmp[tc], in_=banks[bkc][:], func=AF.Sigmoid,
                                                                        bias=bm[:, 8 + j:9 + j])),
                     reads=[("bank", bkc), "bm"], writes=[("tmp", tc)])
                p.op("dve", (lambda e, bya=bya, ta=ta: e.tensor_tensor(out=tmp[ta], in0=banks[bya][:], in1=tmp[ta], op=ALU.mult)),
                     reads=[("bank", bya), ("tmp", ta)], writes=[("tmp", ta)])
                p.op("dve", (lambda e, byc=byc, tc=tc: e.tensor_tensor(out=tmp[tc], in0=banks[byc][:], in1=tmp[tc], op=ALU.mult)),
                     reads=[("bank", byc), ("tmp", tc)], writes=[("tmp", tc)])
                p.op("pool", (lambda e, ta=ta, tc=tc, j=j, tt=tt: e.tensor_tensor(out=yT[:, j, tt * 512:(tt + 1) * 512], in0=tmp[ta],
                                                                                 in1=tmp[tc], op=ALU.add)),
                     reads=[("tmp", ta), ("tmp", tc)],
                     writes=[("yT", tt)] + ([("oraw", h_, b_) for h_ in range(4) for b_ in range(16)]
                                            + [("oT", h_, g_) for h_ in range(4) for g_ in range(4)] if j == 0 else []))

        dbg(p, "gattnT", lambda: gattnT, [128, 4, 2048], BF16, [("gattnT", c, t) for c in range(4) for t in range(4)])
        dbg(p, "gconvT", lambda: gconvT, [128, 4, 2048], BF16, [("gconvT", c, t) for c in range(4) for t in range(4)])
        outs = []
        g_all = [(nm_, c_, t_) for nm_ in ("gattnT", "gconvT") for c_ in range(4) for t_ in range(4)]
        def xres_load(i):
            if i >= 16:
                return
            sb_ = i % NOS
            p.op("sp", (lambda e, i=i, sb_=sb_: e.dma_start(out=xres[sb_], in_=x_perm[i * 128:(i + 1) * 128, :])),
                 writes=[("xres", sb_)] + (g_all if i < NOS else []), dma=f"ldr{sb_}")

        for i in range(NOS - 1):
            xres_load(i)
        for i in range(16):
            sb_ = i % NOS
            xres_load(i + NOS - 1)
            bh = [next_bank(), next_bank()]
            for half in range(2):
                for j in range(KC):
                    p.op("pe", (lambda e, i=i, j=j, half=half, bh=bh: e.matmul(
                        banks[bh[half]][:], lhsT=yT[:, j, i * 128:(i + 1) * 128], rhs=wo[:, j, half * 512:(half + 1) * 512],
                        start=(j == 0), stop=(j == KC - 1))),
                         reads=[("yT", i // 4), "wo"], writes=[("bank", bh[half])])
            for half in range(2):
                p.op("act", (lambda e, half=half, bh=bh, i=i: e.activation(out=junk[:, half * 512:(half + 1) * 512], in_=banks[bh[half]][:],
                                                                          func=AF.Square, accum_out=ss0[:, 2 * i + half:2 * i + half + 1])),
                     reads=[("bank", bh[half])], writes=[("ssf", i, half)])
            p.op("dve", (lambda e, i=i: e.tensor_tensor(out=sd0[:, i:i + 1], in0=ss0[:, 2 * i:2 * i + 1], in1=ss0[:, 2 * i + 1:2 * i + 2],
                                                        op=ALU.add)),
                 reads=[("ssf", i, 0), ("ssf", i, 1)], writes=[("sdf", i)])
            p.op("act", (lambda e, i=i: e.activation(out=sd0[:, i:i + 1], in_=sd0[:, i:i + 1], func=AF.Sqrt, bias=S_EPSD, scale=1.0 / D)),
                 reads=[("sdf", i), "epsd"], writes=[("sdf2", i)])
            p.op("dve", (lambda e, i=i: e.reciprocal(out=rs0[:, i:i + 1], in_=sd0[:, i:i + 1])), reads=[("sdf2", i)], writes=[("rsf", i)])
            for half in range(2):
                p.op("dve", (lambda e, half=half, bh=bh, i=i, sb_=sb_: e.scalar_tensor_tensor(
                    out=ostage[sb_][:, half * 512:(half + 1) * 512], in0=banks[bh[half]][:], scalar=rs0[:, i:i + 1],
                    in1=gpost[:, half * 512:(half + 1) * 512], op0=ALU.mult, op1=ALU.mult)),
                     reads=[("bank", bh[half]), ("rsf", i), "gpost"], writes=[("ost", sb_, half)] + (g_all if i < NOS else []))
            p.op("pool", (lambda e, sb_=sb_: e.tensor_tensor(out=ostage[sb_], in0=ostage[sb_], in1=xres[sb_], op=ALU.add)),
                 reads=[("ost", sb_, 0), ("ost", sb_, 1), ("xres", sb_)], writes=[("ostf", sb_)])
            outs.append(p.op("sp", (lambda e, i=i, sb_=sb_: e.dma_start(out=out_own[i * 128:(i + 1) * 128, :], in_=ostage[sb_])),
                             reads=[("ostf", sb_)], writes=[("ost", sb_, 0), ("ost", sb_, 1)], dma=f"st{sb_}"))
        dbg(p, "yT", lambda: yT, [128, 8, 2048], BF16, [("yT", t) for t in range(4)])
        p.run(final_ops=outs[-NOS:] + dbg_outs)
    return nc


_NC_CACHE = {}


def _get_program(debug=False):
    key = "nc_dbg" if debug else "nc"
    if key not in _NC_CACHE:
        _NC_CACHE[key] = build_program(debug)
    return _NC_CACHE[key]


def _core_layout(r):
    own = [2 * s + r for s in range(4)]
    oth = [2 * s + 1 - r for s in range(4)]
    tiles = own + oth
    pos = np.concatenate([np.arange(512 * t, 512 * t + 512) for t in tiles])
    halo = []
    for t in own:
        for dlt in (2, 1):
            halo.append(512 * t - dlt)
    return own, pos, np.array(halo)


def kernel(x, w_in, lambda_q1, lambda_k1, lambda_q2, lambda_k2, subln_gain, conv_w,
           w_attn_o, w_conv_o, b_merge, w_out, g_pre, g_post, _debug=False):
    x = np.asarray(x, dtype=np.float32)
    f32 = lambda a: np.ascontiguousarray(np.asarray(a, dtype=np.float32))
    w_in0 = f32(w_in)[0]
    w_in_r = np.ascontiguousarray(w_in0.reshape(KC, 128, 48, 128).transpose(2, 1, 0, 3).reshape(48, 128, KC * 128))
    wao_r = np.ascontiguousarray(f32(w_attn_o)[0].reshape(4, 128, D).transpose(1, 0, 2).reshape(128, 4 * D))
    wco_r = np.ascontiguousarray(f32(w_conv_o)[0].reshape(4, 128, D).transpose(1, 0, 2).reshape(128, 4 * D))
    wo_r = np.ascontiguousarray(f32(w_out)[0].reshape(8, 128, D).transpose(1, 0, 2).reshape(128, 8 * D))
    lamv = np.concatenate([f32(lambda_q1)[0], f32(lambda_k1)[0], f32(lambda_q2)[0], f32(lambda_k2)[0]]).reshape(1, 256)
    gsub = f32(subln_gain)[0].reshape(128, 1)
    cw = np.ascontiguousarray(f32(conv_w)[0].reshape(3, 4, 128).transpose(2, 1, 0).reshape(128, 12))
    bm = np.ascontiguousarray(f32(b_merge)[0].reshape(16, 128).T)
    gb = np.ascontiguousarray(np.repeat(f32(g_pre)[0].reshape(8, 128).T[:, :, None], 128, axis=2).reshape(128, KC * 128))
    gpost = f32(g_post)[0].reshape(1, D)
    identf = np.eye(128, dtype=np.float32)
    identb = identf.astype(ml_dtypes.bfloat16)
    kk = np.arange(128)
    tri = np.where(kk[:, None] <= kk[None, :], 0.0, NEG).astype(np.float32).astype(ml_dtypes.bfloat16)

    zeros_bf = np.zeros((64, NT), dtype=ml_dtypes.bfloat16)
    in_maps = []
    layouts = []
    for c in range(8):
        b, r = c // 2, c % 2
        own, pos, halo = _core_layout(r)
        layouts.append((b, own))
        x_perm = np.ascontiguousarray(x[b][pos])
        x_halo = np.zeros((8, D), dtype=np.float32)
        for i, hp in enumerate(halo):
            if hp >= 0:
                x_halo[i] = x[b][hp]
        kidx = np.arange(NT)
        xind = [((kidx >= NOWN + 512 * t) & (kidx < NOWN + 512 * (t + 1))).astype(np.float32) for t in range(4)]
        kaug = np.stack([pos // 128, pos % 128, np.ones_like(pos), np.ones_like(pos)] + xind).astype(np.float32)
        qpos = pos[:NOWN]
        qa = []
        for h in range(4):
            sl = SLOPES[h]
            xm_rows = [np.where((np.arange(NOWN) // 512 == t) & (r == 0), NEG, 0.0) for t in range(4)]
            qa.append(np.stack([np.full(NOWN, 8 * sl * 128), np.full(NOWN, 8 * sl),
                                -8 * sl * 128 * (qpos // 128), -8 * sl * (qpos % 128)] + xm_rows).astype(np.float32))
        qaug = np.concatenate(qa, axis=0)
        xbias = np.full((128, 1), 0.0 if r == 1 else -30000.0, dtype=np.float32)
        in_maps.append(dict(
            x_perm=x_perm, x_halo=x_halo, w_in_r=w_in_r, wao_r=wao_r, wco_r=wco_r, wo_r=wo_r, lamv=lamv, gsub=gsub,
            cw=cw, bm=bm, gb=gb, gpost=gpost, identb=identb, identf=identf, tri=tri,
            kaug=kaug.astype(ml_dtypes.bfloat16), qaug=qaug.astype(ml_dtypes.bfloat16), xbias=xbias, zeros=zeros_bf))

    nc = _get_program(_debug)
    res = run_bass_kernel_spmd(nc, in_maps, core_ids=list(range(8)))
    if _debug:
        _NC_CACHE["last_results"] = res.results
    out = np.empty_like(x)
    for c in range(8):
        b, own = layouts[c]
        oo = res.results[c]["out_own"]
        for s, t in enumerate(own):
            out[b, 512 * t:512 * t + 512] = oo[512 * s:512 * s + 512]
    return out
```
